# Optimizing a Trainium2 kernel written in Bass

```python
import jax, jax.numpy as jnp
from jax import lax
import numpy as np

D_MODEL = 1024
BATCH = 8
SEQ = 2048
DEPTH = 2
DEC_BATCH = 128
DEC_SEQ = 1
PAST_LEN = 16384
PAGE_SIZE = 128

HEAD_DIM = 64
D_MIX = D_MODEL
D_A = 3 * D_MIX // 8
D_C = 3 * D_MIX // 8
D_B = D_MIX - D_A - D_C
POOL_WINDOWS = (2, 4, 8, 16)
N_POOL_GROUPS = len(POOL_WINDOWS)
POOL_GROUP_DIM = D_B // N_POOL_GROUPS
POOL_STATE = max(POOL_WINDOWS) - 1
CONV_A_WIDTH = 31
CONV_C_WIDTH = 3
D_IN = 2 * D_A + D_B + 3 * D_C
D_FF = 4 * D_MODEL
D_PLE = 256
EPS = 1e-6

kernel_name = "hybrid_conv_pool_shortconv_decoder_step"


def rms_norm(x, g):
    x32 = x.astype(jnp.float32)
    y = x32 * lax.rsqrt(jnp.mean(x32 * x32, axis=-1, keepdims=True) + EPS)
    return (y * g.astype(jnp.float32)).astype(x.dtype)


def layer_norm(x, g, b):
    x32 = x.astype(jnp.float32)
    mu = jnp.mean(x32, axis=-1, keepdims=True)
    xc = x32 - mu
    var = jnp.mean(xc * xc, axis=-1, keepdims=True)
    y = xc * lax.rsqrt(var + EPS)
    return (y * g.astype(jnp.float32) + b.astype(jnp.float32)).astype(x.dtype)


def causal_depthwise_conv(xe, w):
    c = xe.shape[-1]
    rhs = w.astype(xe.dtype)[:, None, :]
    return lax.conv_general_dilated(xe, rhs, window_strides=(1,), padding="VALID",
                                    dimension_numbers=("NWC", "WIO", "NWC"),
                                    feature_group_count=c)


def multiscale_pool(ue, pos0):
    t_new = ue.shape[1] - POOL_STATE
    u32 = ue.astype(jnp.float32)
    cs = jnp.pad(jnp.cumsum(u32, axis=1), ((0, 0), (1, 0), (0, 0)))
    end = cs[:, POOL_STATE + 1:]
    pos = pos0 + jnp.arange(t_new, dtype=jnp.int32)
    parts = []
    for g, w in enumerate(POOL_WINDOWS):
        sl = slice(g * POOL_GROUP_DIM, (g + 1) * POOL_GROUP_DIM)
        start = cs[:, POOL_STATE + 1 - w: POOL_STATE + 1 - w + t_new, sl]
        count = jnp.minimum(pos + 1, w).astype(jnp.float32)[None, :, None]
        parts.append((end[..., sl] - start) / count)
    pooled = jnp.concatenate(parts, axis=-1)
    return (pooled - u32[:, POOL_STATE:]).astype(ue.dtype)


def trunk_layer(h, p_i, st_a, st_p, st_c, pos0,
                g_mix, w_in, conv_a_w, conv_a_b, ln_a_g, ln_a_b, pool_w, pool_scale,
                conv_c_w, w_out, g_mlp, w_up, w_down, g_ple, w_ple_gate, w_ple_proj):
    bsz, t_new, _ = h.shape
    n = rms_norm(h, g_mix)
    z = n @ w_in
    a_v, a_g, u_b, c_b, c_c, c_x = jnp.split(
        z, np.cumsum([D_A, D_A, D_B, D_C, D_C]).tolist(), axis=-1)

    glu = a_v * jax.nn.sigmoid(a_g)
    ge = jnp.concatenate([st_a, glu], axis=1)
    ya = causal_depthwise_conv(ge, conv_a_w) + conv_a_b
    out_a = jax.nn.silu(layer_norm(ya, ln_a_g, ln_a_b))
    new_a = ge[:, -(CONV_A_WIDTH - 1):]

    ue = jnp.concatenate([st_p, u_b], axis=1)
    pooled = multiscale_pool(ue, pos0).reshape(bsz, t_new, N_POOL_GROUPS, POOL_GROUP_DIM)
    out_b = jnp.einsum("btgc,gcd->btgd", pooled, pool_w).reshape(bsz, t_new, D_B) * pool_scale
    new_p = ue[:, -POOL_STATE:]

    ve = jnp.concatenate([st_c, c_c * c_x], axis=1)
    out_c = c_b * causal_depthwise_conv(ve, conv_c_w)
    new_c = ve[:, -(CONV_C_WIDTH - 1):]

    h = h + jnp.concatenate([out_a, out_b, out_c], axis=-1) @ w_out

    m = rms_norm(h, g_mlp)
    h = h + jnp.square(jax.nn.relu(m @ w_up)) @ w_down

    gate = jax.nn.sigmoid(rms_norm(h, g_ple) @ w_ple_gate)
    h = h + gate * (p_i @ w_ple_proj)
    return h, new_a, new_p, new_c


def setup_inputs(seed: int = 0) -> dict:
    key = jax.random.key(seed)
    ks = jax.random.split(key, 32)
    f32 = jnp.float32
    nrm = lambda k, shape, s: jax.random.normal(k, shape, f32) * s
    return {
        "x_prompt": nrm(ks[0], (BATCH, SEQ, D_MODEL), 1.0),
        "x_sample": nrm(ks[1], (DEC_BATCH, DEC_SEQ, D_MODEL), 1.0),
        "state_conv_a": nrm(ks[2], (DEPTH, DEC_BATCH, CONV_A_WIDTH - 1, D_A), 1.0),
        "state_pool": nrm(ks[3], (DEPTH, DEC_BATCH, POOL_STATE, D_B), 1.0),
        "state_conv_c": nrm(ks[4], (DEPTH, DEC_BATCH, CONV_C_WIDTH - 1, D_C), 1.0),
        "p_prompt": nrm(ks[5], (DEPTH, BATCH, SEQ, D_PLE), 1.0),
        "p_sample": nrm(ks[6], (DEPTH, DEC_BATCH, DEC_SEQ, D_PLE), 1.0),
        "norm_mix_g": 1.0 + nrm(ks[7], (DEPTH, D_MODEL), 0.05),
        "w_in": nrm(ks[8], (DEPTH, D_MODEL, D_IN), D_MODEL ** -0.5),
        "conv_a_w": nrm(ks[9], (DEPTH, CONV_A_WIDTH, D_A), CONV_A_WIDTH ** -0.5),
        "conv_a_b": nrm(ks[10], (DEPTH, D_A), 0.02),
        "ln_a_g": 1.0 + nrm(ks[11], (DEPTH, D_A), 0.05),
        "ln_a_b": nrm(ks[12], (DEPTH, D_A), 0.02),
        "pool_w": nrm(ks[13], (DEPTH, N_POOL_GROUPS, POOL_GROUP_DIM, POOL_GROUP_DIM), POOL_GROUP_DIM ** -0.5),
        "pool_scale": 1.0 + nrm(ks[14], (DEPTH, D_B), 0.1),
        "conv_c_w": nrm(ks[15], (DEPTH, CONV_C_WIDTH, D_C), CONV_C_WIDTH ** -0.5),
        "w_out": nrm(ks[16], (DEPTH, D_MIX, D_MODEL), D_MIX ** -0.5),
        "norm_mlp_g": 1.0 + nrm(ks[17], (DEPTH, D_MODEL), 0.05),
        "w_up": nrm(ks[18], (DEPTH, D_MODEL, D_FF), D_MODEL ** -0.5),
        "w_down": nrm(ks[19], (DEPTH, D_FF, D_MODEL), D_FF ** -0.5),
        "norm_ple_g": 1.0 + nrm(ks[20], (DEPTH, D_MODEL), 0.05),
        "w_ple_gate": nrm(ks[21], (DEPTH, D_MODEL, D_MODEL), D_MODEL ** -0.5),
        "w_ple_proj": nrm(ks[22], (DEPTH, D_PLE, D_MODEL), D_PLE ** -0.5),
        "final_norm_g": 1.0 + nrm(ks[23], (D_MODEL,), 0.05),
    }


def reference(x_prompt, x_sample, state_conv_a, state_pool, state_conv_c, p_prompt, p_sample,
              norm_mix_g, w_in, conv_a_w, conv_a_b, ln_a_g, ln_a_b, pool_w, pool_scale, conv_c_w,
              w_out, norm_mlp_g, w_up, w_down, norm_ple_g, w_ple_gate, w_ple_proj, final_norm_g):
    bp = x_prompt.shape[0]
    dt = x_prompt.dtype
    zero_a = jnp.zeros((bp, CONV_A_WIDTH - 1, D_A), dt)
    zero_p = jnp.zeros((bp, POOL_STATE, D_B), dt)
    zero_c = jnp.zeros((bp, CONV_C_WIDTH - 1, D_C), dt)
    hp, hs = x_prompt, x_sample
    pa, pp, pc, sa, sp, sc = [], [], [], [], [], []
    for i in range(DEPTH):
        lw = (norm_mix_g[i], w_in[i], conv_a_w[i], conv_a_b[i], ln_a_g[i], ln_a_b[i], pool_w[i],
              pool_scale[i], conv_c_w[i], w_out[i], norm_mlp_g[i], w_up[i], w_down[i],
              norm_ple_g[i], w_ple_gate[i], w_ple_proj[i])
        hp, a_i, p_i, c_i = trunk_layer(hp, p_prompt[i], zero_a, zero_p, zero_c, 0, *lw)
        hs, a_j, p_j, c_j = trunk_layer(hs, p_sample[i], state_conv_a[i], state_pool[i],
                                        state_conv_c[i], PAST_LEN, *lw)
        pa.append(a_i); pp.append(p_i); pc.append(c_i)
        sa.append(a_j); sp.append(p_j); sc.append(c_j)
    y_prompt = rms_norm(hp, final_norm_g)
    y_sample = rms_norm(hs, final_norm_g)
    new_conv_a_prompt = jnp.stack(pa)
    new_pool_prompt = jnp.stack(pp)
    new_conv_c_prompt = jnp.stack(pc)
    new_conv_a_sample = jnp.stack(sa)
    new_pool_sample = jnp.stack(sp)
    new_conv_c_sample = jnp.stack(sc)
    return (y_prompt, y_sample, new_conv_a_prompt, new_pool_prompt, new_conv_c_prompt,
            new_conv_a_sample, new_pool_sample, new_conv_c_sample)
```

```python
import numpy as np
from contextlib import ExitStack
import concourse.bass as bass
import concourse.mybir as mybir
from concourse.bass_utils import run_bass_kernel_spmd

F32 = mybir.dt.float32
BF16 = mybir.dt.bfloat16
ALU = mybir.AluOpType
AF = mybir.ActivationFunctionType
AX = mybir.AxisListType

NCORES = 8
D = 1024
SEQ = 2048
NS_TOK = 16
TP = SEQ + NS_TOK
SEGW = 1040
DA, DB, DC = 384, 256, 384
DIN = 2176
DFF = 4096
DPLE = 256
DEPTH = 2
EPS = 1e-6
WIN = (2, 4, 8, 16)
NSLOT = 8

R_GMIX, R_GMLP, R_GPLE, R_GFIN = 0, 2, 4, 6
R_CAB, R_LNG, R_LNB, R_PSC, R_CCW = 7, 9, 11, 13, 15
NPROW = 24


class Buf:
    __slots__ = ("name", "w", "r", "lo", "hi", "over")

    def __init__(self, name, lo=None, hi=None):
        self.name = name
        self.w = None
        self.r = {}
        self.lo = lo
        self.hi = hi
        self.over = []


class Tracker:
    def __init__(self, nc, es, n_dma_sems=10):
        self.nc = nc
        self.dry = False
        self.eng = {"pe": nc.tensor, "act": nc.scalar, "dve": nc.vector, "pool": nc.gpsimd, "sp": nc.sync}
        self.sem = {}
        self.cnt = {}
        for e in ("pe", "act", "dve", "pool"):
            self.sem[e] = es.enter_context(nc.semaphore("s_" + e))
            self.cnt[e] = 0
        self.dsem = {}
        for q in ("pool", "sp"):
            self.dsem[q] = [[es.enter_context(nc.semaphore("d_%s%d" % (q, i))), 0] for i in range(n_dma_sems)]
        self.dnext = {"pool": 0, "sp": 0}
        self.known = {e: {} for e in self.eng}
        self.arena = []
        self.nwait = 0
        self.nins = 0

    def abuf(self, name, lo, hi):
        b = Buf(name, lo, hi)
        for o in self.arena:
            if o.lo < hi and lo < o.hi:
                o.over.append(b)
                b.over.append(o)
        self.arena.append(b)
        return b

    def _wait(self, e, evs):
        best = {}
        kn = self.known[e]
        for (sem, val, src) in evs:
            k = id(sem)
            if kn.get(k, 0) >= val:
                continue
            if k not in best or best[k][1] < val:
                best[k] = (sem, val)
        for k, (sem, val) in best.items():
            self.eng[e].wait_ge(sem, val)
            kn[k] = val
            self.nwait += 1

    def deps(self, e, reads, writes):
        evs = []
        for b in reads:
            for x in [b] + b.over:
                if x.w is not None:
                    evs.append(x.w)
        for b in writes:
            for x in [b] + b.over:
                if x.w is not None and (x.w[2] != e or e != "pe"):
                    evs.append(x.w)
                for r in x.r.values():
                    if r[2] != e or e != "pe":
                        evs.append(r)
        return evs

    def record(self, ev, reads, writes):
        k = id(ev[0])
        for b in reads:
            b.r[k] = ev
        for b in writes:
            b.w = ev
            b.r = {}

    def op(self, e, reads, writes, emit):
        if self.dry:
            return None
        self._wait(e, self.deps(e, reads, writes))
        ins = emit()
        self.cnt[e] += 1
        self.nins += 1
        ins.then_inc(self.sem[e], 1)
        ev = (self.sem[e], self.cnt[e], e)
        self.record(ev, reads, writes)
        return ev

    def dma(self, q, reads, writes, out, in_, **kw):
        if self.dry:
            return None
        i = self.dnext[q]
        self.dnext[q] = (i + 1) % len(self.dsem[q])
        slot = self.dsem[q][i]
        evs = self.deps("dma:" + q, reads, writes)
        if slot[1] > 0:
            evs.append((slot[0], slot[1], "dma"))
        self._wait(q, evs)
        slot[1] += 16
        self.eng[q].dma_start(out=out, in_=in_, **kw).then_inc(slot[0], 16)
        ev = (slot[0], slot[1], "dma")
        self.record(ev, reads, writes)
        return ev


class TileD:
    def __init__(self, idx, h0, w, lo, samp, li):
        self.idx, self.h0, self.w, self.lo, self.samp, self.li = idx, h0, w, lo, samp, li


SEG_TILES = [
    [TileD(0, 0, 512, 0, False, 0), TileD(1, 512, 512, 512, False, 1)],
    [TileD(2, 1024, 512, 0, False, 0), TileD(3, 1536, 512, 512, False, 1), TileD(4, 2048, NS_TOK, 1024, True, 2)],
]


class Builder:
    def __init__(self):
        self.nc = bass.Bass("TRN2", target_bir_lowering=False)
        self.es = ExitStack()

    def dram_in(self, name, shape):
        return self.nc.dram_tensor(name, list(shape), F32, kind="ExternalInput").ap()

    def dram_out(self, name, shape):
        return self.nc.dram_tensor(name, list(shape), F32, kind="ExternalOutput").ap()

    def sb(self, name, shape, dt):
        return self.es.enter_context(self.nc.sbuf_tensor(name, list(shape), dt))

    def declare(self):
        di, do = self.dram_in, self.dram_out
        self.xp = di("xp", [SEQ, D]); self.xs = di("xs", [NS_TOK, D])
        self.sca = di("sca", [DEPTH, NS_TOK, 30, DA]); self.spl = di("spl", [DEPTH, NS_TOK, 15, DB])
        self.scc = di("scc", [DEPTH, NS_TOK, 2, DC])
        self.pp = di("pp", [DEPTH, SEQ, DPLE]); self.psm = di("psm", [DEPTH, NS_TOK, DPLE])
        self.g_mix = di("g_mix", [DEPTH, D]); self.g_mlp = di("g_mlp", [DEPTH, D]); self.g_ple = di("g_ple", [DEPTH, D])
        self.g_fin = di("g_fin", [1, D])
        self.caw = di("caw", [DEPTH, 31, DA]); self.cab = di("cab", [DEPTH, DA])
        self.lng = di("lng", [DEPTH, DA]); self.lnb = di("lnb", [DEPTH, DA])
        self.plw = di("plw", [DEPTH, 4, 64, 64]); self.psc = di("psc", [DEPTH, DB])
        self.ccw = di("ccw", [DEPTH, 3, DC])
        self.wd = {
            "w_in": di("w_in", [DEPTH, D, DIN]), "w_out": di("w_out", [DEPTH, D, D]),
            "w_up": di("w_up", [DEPTH, D, DFF]), "w_down": di("w_down", [DEPTH, DFF, D]),
            "w_pg": di("w_pg", [DEPTH, D, D]), "w_pp": di("w_pp", [DEPTH, DPLE, D]),
        }
        self.yp = do("yp", [SEQ, D]); self.ys = do("ys", [NS_TOK, D])
        self.nap = do("nap", [DEPTH, 30, DA]); self.npp = do("npp", [DEPTH, 15, DB]); self.ncp = do("ncp", [DEPTH, 2, DC])
        self.nas = do("nas", [DEPTH, NS_TOK, 30, DA]); self.nps = do("nps", [DEPTH, NS_TOK, 15, DB])
        self.ncs = do("ncs", [DEPTH, NS_TOK, 2, DC])

    def alloc(self):
        nc, sb = self.nc, self.sb
        self.T = Tracker(nc, self.es)
        T = self.T
        self.H = sb("H", [128, 8, TP], F32)
        self.Hb = [[Buf("H%d_%d" % (c, t)) for t in range(5)] for c in range(8)]
        self.N = sb("N", [128, 8, SEGW], BF16)
        self.Nb = [[Buf("N%d_%d" % (c, t)) for t in range(3)] for c in range(8)]
        self.R = sb("R", [128, 32 * SEGW], BF16)
        self.HID = self.R[:, :].rearrange("p (c t) -> p c t", c=32)
        self.HIDb = [[T.abuf("HID%d_%d" % (m, t.li), (m * SEGW + t.lo) * 2, (m * SEGW + t.lo + t.w) * 2)
                      for t in SEG_TILES[1]] for m in range(32)]
        self._off = 0

        def carve(shape, dt):
            n = 1
            for s in shape:
                n *= s
            esz = 4 if dt == F32 else 2
            lo = self._off
            hi = lo + n * esz
            self._off = (hi + 31) // 32 * 32
            assert self._off <= 32 * SEGW * 2, self._off
            v = self.R[:, lo // 2:hi // 2]
            if dt == F32:
                v = v.bitcast(F32)
            if len(shape) == 2:
                v = v.rearrange("p (a b) -> p a b", a=shape[0])
            return v, lo, hi

        self.MIX, lo, hi = carve([8, SEGW], BF16)
        self.MIXb = [[T.abuf("MIX%d_%d" % (c, t.li), lo + (c * SEGW + t.lo) * 2, lo + (c * SEGW + t.lo + t.w) * 2)
                      for t in SEG_TILES[1]] for c in range(8)]
        self.GBS, lo, hi = carve([3, 30 + 1024], BF16)
        self.GPXb = [T.abuf("GPX%d" % j, lo + j * 1054 * 2, lo + (j * 1054 + 30) * 2) for j in range(3)]
        self.GPb = [[T.abuf("GP%d_%d" % (j, li), lo + (j * 1054 + 30 + li * 512) * 2, lo + (j * 1054 + 30 + li * 512 + 512) * 2)
                     for li in range(2)] for j in range(3)]
        self.VS, lo, hi = carve([2 + 1024], F32)
        self.VSb = T.abuf("VS", lo, hi)
        self.US, lo, hi = carve([15 + 1024], F32)
        self.USb = T.abuf("US", lo, hi)
        self.TT = []
        self.TTb = []
        for i in range(4):
            v, lo, hi = carve([527], F32)
            self.TT.append(v); self.TTb.append(T.abuf("TT%d" % i, lo, hi))
            self._tt_lo = lo
            if i == 0:
                self._tt0_lo = lo
        self.SIG = []
        self.SIGb = []
        for i in range(2):
            v, lo, hi = carve([512], F32)
            self.SIG.append(v); self.SIGb.append(T.abuf("SIG%d" % i, lo, hi))
        self.signext = 0
        self.YA, self.YAb, self.YS, self.YSb, self.PLD, self.PLDb = [], [], [], [], [], []
        self._ya_lo, self._ys_lo = [], []
        for li in range(2):
            v, lo, hi = carve([3, 512], F32)
            self.YA.append(v); self.YAb.append(T.abuf("YA%d" % li, lo, hi))
            self._ya_lo.append(lo)
            if li == 1:
                self.PS = self.R[:, lo // 2:(lo + 4096) // 2].bitcast(F32).rearrange("p (s d) -> p s d", s=4)
                self.PSb = T.abuf("PS", lo, lo + 4096)
            v, lo, hi = carve([3, 512], BF16)
            self.YS.append(v); self.YSb.append(T.abuf("YS%d" % li, lo, hi))
            self._ys_lo.append(lo)
            v, lo, hi = carve([2, 512], BF16)
            self.PLD.append(v); self.PLDb.append(T.abuf("PLD%d" % li, lo, hi))
        def ov(name, lo, np_, shape):
            n = 1
            for q in shape:
                n *= q
            v = self.R[0:np_, lo // 2:(lo + n * 4) // 2].bitcast(F32)
            if len(shape) == 2:
                v = v.rearrange("p (a b) -> p a b", a=shape[0])
            return v, T.abuf(name, lo, lo + n * 4)
        self.PST, self.PSTb = ov("PST", 32768, NPROW, [D])
        self.CAWS, self.CAWSb = ov("CAWS", 36864, 32, [DEPTH, DA])
        self.XSS, self.XSSb = ov("XSS", 40960, 16, [D])
        self.PSS, self.PSSb = ov("PSS", self._ya_lo[1] + 4096, 16, [DPLE])
        self.SST, self.SSTb = ov("SST", self._ys_lo[1], 128, [DA])
        self.OST, self.OSTb = ov("OST", self._ya_lo[0], 46, [DA])
        ost1, ost1b = ov("OST1", self._ya_lo[0] + 1536, 46, [DA])
        ost2, ost2b = ov("OST2", self._ya_lo[0] + 3072, 46, [DA])
        self.OSTL = [(self.OST, self.OSTb), (ost1, ost1b), (ost2, ost2b)]
        self.TMPS, self.TMPSb = ov("TMPS", self._tt_lo, 128, [4, 30])
        ps2, ps2b = ov("PS2", self._tt0_lo, 128, [4, 256])
        self.PSL = [(self.PS, self.PSb), (ps2, ps2b)]
        self.SST2 = []
        for q in range(2):
            v_, b_ = ov("SSTq%d" % q, self._ys_lo[1] + q * 1536, 128, [DA])
            self.SST2.append((v_, b_))
        self.SQX = self.R[:, 45056 // 2:(45056 + 8192) // 2].rearrange("p (c t) -> p c t", c=8)
        self.SQXb = [T.abuf("SQX%d" % c, 45056 + c * 1024, 45056 + (c + 1) * 1024) for c in range(8)]
        self.XST = [self.R[:, i * 8192:(i + 1) * 8192].bitcast(F32).rearrange("p (s d) -> p s d", s=4) for i in range(2)]
        self.XSTb = [T.abuf("XST%d" % i, i * 16384, (i + 1) * 16384) for i in range(2)]
        self.YSTb = [T.abuf("YST%d" % q, q * 4096, (q + 1) * 4096) for q in range(4)]
        self.ystnext = 0
        self.WS = [sb("WS%d" % i, [128, 8, 128], BF16) for i in range(NSLOT)]
        self.WSb = [Buf("WS%d" % i) for i in range(NSLOT)]
        self.DIAG = [sb("DIAG%d" % i, [128, 31, 128], BF16) for i in range(3)]
        self.DIAGb = [Buf("DIAG%d" % i) for i in range(3)]
        self.PT = sb("PT", [128, 2, SEGW], BF16)
        self.PTb = [Buf("PT%d" % i) for i in range(3)]
        self.PARAM = sb("PARAM", [128, 8, NPROW], F32); self.PARAMb = Buf("PARAM")
        self.PARAMN = sb("PARAMN", [128, 3, 4], F32)
        self.CST = sb("CST", [128, 2], F32); self.CSTb = Buf("CST")
        self.CAWT = sb("CAWT", [128, DEPTH, 3, 31], F32); self.CAWTb = Buf("CAWT")
        self.IDF = sb("IDF", [128, 128], F32); self.IDFb = Buf("IDF")
        self.IDB = sb("IDB", [128, 128], BF16); self.IDBb = Buf("IDB")
        self.ONF = sb("ONF", [128, 128], F32); self.ONB = sb("ONB", [128, 128], BF16); self.ONb = Buf("ON")
        self.PW = sb("PW", [128, DEPTH, 2, 128], BF16); self.PWb = Buf("PW")
        self.INVW = sb("INVW", [128, 2], F32); self.INVC = sb("INVC", [128, 2, 16], F32); self.INVb = Buf("INV")
        self.RS = [sb("RS%d" % i, [128, 512], F32) for i in range(3)]
        self.RSb = [Buf("RS%d" % i) for i in range(3)]
        self.rsnext = 0
        self.RL = [sb("RL%d" % i, [128, 512], F32) for i in range(2)]
        self.RLb = [Buf("RL%d" % i) for i in range(2)]
        self.rlnext = 0
        self.SQ = [sb("SQ%d" % i, [128, 512], BF16) for i in range(2)]
        self.SQb = [Buf("SQ%d" % i) for i in range(2)]
        self.sqnext = 0
        self.GC = sb("GC", [128, 3, 30], BF16); self.GCb = Buf("GC")
        self.VC = sb("VC", [128, 3, 2], F32); self.VCb = Buf("VC")
        self.UC = sb("UC", [128, 2, 15], F32); self.UCb = Buf("UC")
        self.STA = sb("STA", [128, 3, 46], F32); self.STAb = Buf("STA")
        self.STU = sb("STU", [128, 2, 31], F32); self.STUb = Buf("STU")
        self.STV = sb("STV", [128, 3, 18], F32); self.STVb = Buf("STV")
        self.YAS = sb("YAS", [128, 3, 16], F32); self.YASb = Buf("YAS")
        self.YSS = sb("YSS", [128, 3, 16], BF16); self.YSSb = Buf("YSS")
        self.SSS = sb("SSS", [128, 2, 16], F32); self.SSSb = Buf("SSS")
        self.PLS = sb("PLS", [128, 2, 16], BF16); self.PLSb = Buf("PLS")
        self.SCT = sb("SCT", [128, 3, 32], F32); self.SCTb = Buf("SCT")
        self.SMT = sb("SMT", [128, 3, 16], F32); self.SMTb = Buf("SMT")
        self.LNS = sb("LNS", [128, 2, 16], F32); self.LNSb = [Buf("LNS0"), Buf("LNS1")]
        self.PB = [self.es.enter_context(nc.psum_tensor("pb%d" % i, [128, 512], F32)) for i in range(8)]
        self.PBb = [Buf("pb%d" % i) for i in range(8)]
        self.pbnext = 0
        self.out_evs = []
        self.deferred = []
        self.deferred_lo = []
        self._xsubs = {}
        self.worder = []
        self.wpos = 0
        self.wissued = 0
        self.wreleased = 0

    def bank(self):
        i = self.pbnext
        self.pbnext = (i + 1) % 8
        return self.PB[i], self.PBb[i]

    def ring(self, which):
        if which == "sig":
            i = self.signext; self.signext = (i + 1) % 2
            return self.SIG[i], self.SIGb[i]
        if which == "rs":
            i = self.rsnext; self.rsnext = (i + 1) % 3
            return self.RS[i], self.RSb[i]
        if which == "rl":
            i = self.rlnext; self.rlnext = (i + 1) % 2
            return self.RL[i], self.RLb[i]
        i = self.sqnext; self.sqnext = (i + 1) % 2
        return self.SQ[i], self.SQb[i]

    def w_acquire(self, spec):
        if self.T.dry:
            self.worder.append(spec)
            return 0
        idx = self.wpos
        self.wpos += 1
        assert self.worder[idx] == spec, (idx, self.worder[idx], spec)
        self.w_pump()
        assert self.wissued > idx
        return idx % NSLOT

    def w_release(self, n=1):
        if self.T.dry:
            return
        self.wreleased += n
        self.w_pump()

    def w_pump(self):
        while self.wissued < len(self.worder) and self.wissued < self.wreleased + NSLOT:
            i = self.wissued
            name, l, r0, kc, c0 = self.worder[i]
            s = i % NSLOT
            src = self.wd[name][l, r0:r0 + kc * 128, c0:c0 + 128].rearrange("(kc p) n -> p kc n", p=128)
            self.T.dma("pool", [], [self.WSb[s]], self.WS[s][:, 0:kc, :], src)
            self.wissued += 1

    def prow(self, c, row):
        return self.PARAM[:, c, row:row + 1]

    def nprow(self, c, row):
        if R_LNG <= row < R_LNG + DEPTH:
            k = row - R_LNG
        else:
            k = 2 + row - R_LNB
        return self.PARAMN[:, c, k:k + 1]

    def act_sigmoid(self, out, outb, in_, inbufs, nscale=None, nbias=None):
        nc, T = self.nc, self.T
        kw = {}
        if nbias is not None:
            kw["bias"] = nbias
        sc = nscale if nscale is not None else -1.0
        T.op("act", list(inbufs) + [self.PARAMb], [outb], lambda: nc.scalar.activation(out, in_, AF.Exp, scale=sc, **kw))
        T.op("act", [outb, self.CSTb], [outb], lambda: nc.scalar.activation(out, out, AF.Ln, bias=self.CST[:, 1:2]))
        T.op("act", [outb], [outb], lambda: nc.scalar.activation(out, out, AF.Exp, scale=-1.0))

    def act_rsqrt(self, out, outb, in_, inbufs, scale):
        nc, T = self.nc, self.T
        T.op("act", list(inbufs) + [self.CSTb], [outb], lambda: nc.scalar.activation(
            out, in_, AF.Ln, scale=scale, bias=self.CST[:, 0:1]))
        T.op("act", [outb], [outb], lambda: nc.scalar.activation(out, out, AF.Exp, scale=-0.5))

    def mm(self, slot, tile, nk=8, rhs=None, rbufs=None, korder=None, perk=False):
        nc = self.nc
        bk, bb = self.bank()
        w, lo = tile.w, tile.lo
        if rhs is None:
            rhs = lambda k: self.N[:, k, lo:lo + w]
            rbufs = [self.Nb[k][tile.li] for k in range(nk)]
        ko = korder if korder is not None else list(range(nk))
        if perk and len(rbufs) == nk:
            for n_, k in enumerate(ko):
                self.T.op("pe", [self.WSb[slot], rbufs[k]], [bb], lambda n_=n_, k=k: nc.tensor.matmul(
                    bk[:, 0:w], self.WS[slot][:, k, :], rhs(k), start=(n_ == 0), stop=(n_ == nk - 1)))
            return bk, bb

        def emit():
            last = None
            for n_, k in enumerate(ko):
                last = nc.tensor.matmul(bk[:, 0:w], self.WS[slot][:, k, :], rhs(k), start=(n_ == 0), stop=(n_ == nk - 1))
            return last
        self.T.op("pe", [self.WSb[slot]] + rbufs, [bb], emit)
        return bk, bb

    def build_identity(self):
        nc, T = self.nc, self.T
        T.op("pool", [], [self.IDFb], lambda: nc.gpsimd.memset(self.IDF[:], 1.0))
        T.op("pool", [self.IDFb], [self.IDFb], lambda: nc.gpsimd.affine_select(
            out=self.IDF[:], in_=self.IDF[:], pattern=[[-1, 128]], compare_op=ALU.is_equal, fill=0.0,
            base=0, channel_multiplier=1))

    def prologue(self):
        nc, T = self.nc, self.T
        T.op("dve", [], [self.PSTb], lambda: nc.vector.memset(self.PST[:], 0.0))
        T.op("dve", [], [self.CAWSb], lambda: nc.vector.memset(self.CAWS[:], 0.0))
        self.load_x(range(1), phase="dma")
        groups = [(R_GMIX, self.g_mix, DEPTH, D), (R_GMLP, self.g_mlp, DEPTH, D), (R_GPLE, self.g_ple, DEPTH, D),
                  (R_GFIN, self.g_fin, 1, D), (R_CAB, self.cab, DEPTH, DA), (R_LNG, self.lng, DEPTH, DA),
                  (R_LNB, self.lnb, DEPTH, DA), (R_PSC, self.psc, DEPTH, DB),
                  (R_CCW, self.ccw.rearrange("l k c -> (l k) c"), DEPTH * 3, DC)]
        self.PSTrb = []
        for (r, src, nr, n) in groups:
            rb_ = Buf("PSTr%d" % r)
            rb_.w = self.PSTb.w
            self.PSTrb.append(rb_)
            T.dma("sp", [], [rb_], self.PST[r:r + nr, 0:n], src)
        self.CAWSrb = []
        for l in range(DEPTH):
            rb_ = Buf("CAWSr%d" % l)
            rb_.w = self.CAWSb.w
            self.CAWSrb.append(rb_)
            T.dma("sp", [], [rb_], self.CAWS[0:31, l, :], self.caw[l])
        T.op("dve", [self.IDFb], [self.IDBb], lambda: nc.vector.tensor_copy(self.IDB[:], self.IDF[:]))
        T.op("dve", [], [self.ONb], lambda: nc.vector.memset(self.ONF[:], 1.0))
        T.op("dve", [], [self.ONb], lambda: nc.vector.memset(self.ONB[:], 1.0))
        for i, (wa, wb) in enumerate(((2, 4), (8, 16))):
            T.op("dve", [], [self.INVb], lambda i=i, wa=wa: nc.vector.memset(self.INVW[0:64, i:i + 1], 1.0 / wa))
            T.op("dve", [], [self.INVb], lambda i=i, wb=wb: nc.vector.memset(self.INVW[64:128, i:i + 1], 1.0 / wb))
        for t in range(16):
            T.op("dve", [], [self.INVb], lambda t=t: nc.vector.memset(self.INVC[:, :, t:t + 1], 1.0 / (t + 1)))
        for i in range(2):
            T.op("dve", [self.INVb], [self.INVb], lambda i=i: nc.vector.tensor_scalar(
                self.INVC[:, i, :], self.INVC[:, i, :], self.INVW[:, i:i + 1], None, op0=ALU.max))
        for c in range(8):
            bk, bb = self.bank()
            T.op("pe", [self.PSTb, self.IDFb] + self.PSTrb, [bb], lambda c=c, bk=bk: nc.tensor.transpose(
                bk[:, 0:NPROW], self.PST[:, c * 128:(c + 1) * 128], self.IDF[0:NPROW, 0:NPROW]))
            T.op("act", [bb], [self.PARAMb], lambda c=c, bk=bk: nc.scalar.copy(self.PARAM[:, c, :], bk[:, 0:NPROW]))
        T.op("dve", [], [self.CSTb], lambda: nc.vector.memset(self.CST[:, 0:1], EPS))
        T.op("dve", [], [self.CSTb], lambda: nc.vector.memset(self.CST[:, 1:2], 1.0))
        T.op("dve", [self.PARAMb], [self.PARAMb], lambda: nc.vector.tensor_scalar(
            self.PARAMN[:, :, 0:2], self.PARAM[:, 0:3, R_LNG:R_LNG + 2], -1.0, None, op0=ALU.mult))
        T.op("dve", [self.PARAMb], [self.PARAMb], lambda: nc.vector.tensor_scalar(
            self.PARAMN[:, :, 2:4], self.PARAM[:, 0:3, R_LNB:R_LNB + 2], -1.0, None, op0=ALU.mult))
        for l in range(DEPTH):
            for j in range(3):
                bk, bb = self.bank()
                T.op("pe", [self.CAWSb, self.IDFb] + self.CAWSrb, [bb], lambda l=l, j=j, bk=bk: nc.tensor.transpose(
                    bk[:, 0:32], self.CAWS[:, l, j * 128:(j + 1) * 128], self.IDF[0:32, 0:32]))
                T.op("act", [bb], [self.CAWTb], lambda l=l, j=j, bk=bk: nc.scalar.copy(self.CAWT[:, l, j, :], bk[:, 0:31]))
        T.op("dve", [], [self.PWb], lambda: nc.vector.memset(self.PW[:], 0.0))
        for l in range(DEPTH):
            for g in range(4):
                i, h = g // 2, g % 2
                T.dma("pool", [], [self.PWb], self.PW[h * 64:(h + 1) * 64, l, i, h * 64:(h + 1) * 64], self.plw[l, g])
        self.load_x(range(1), phase="tr")
        self.rmsnorm([SEG_TILES[0][0]], R_GMIX + 0, presq="xnow")
        self.load_x(range(1, 2))
        self.rmsnorm([SEG_TILES[0][1]], R_GMIX + 0, presq="xnow")

    def load_x(self, trange, sample=False, phase="both"):
        nc, T = self.nc, self.T
        for t in trange:
            st, stb = self.XST[t % 2], self.XSTb[t % 2]
            if phase in ("both", "dma"):
                subs = []
                for q in range(4):
                    sbf = Buf("XSTs%d_%d" % (t, q))
                    sbf.w = stb.w
                    subs.append(sbf)
                for q in range(4):
                    if q == 0:
                        ev0 = T.dma("sp", [], [stb], st[:, q, :], self.xp[t * 512 + q * 128:t * 512 + (q + 1) * 128, :])
                        subs[0].w = ev0
                    else:
                        T.dma("sp", [], [subs[q]], st[:, q, :], self.xp[t * 512 + q * 128:t * 512 + (q + 1) * 128, :])
                self._xsubs[t] = subs
                if phase == "dma":
                    continue
            subs = self._xsubs[t]
            for c in range(8):
                bk, bb = self.bank()

                def emit(c=c, bk=bk, st=st):
                    last = None
                    for s in range(4):
                        last = nc.tensor.transpose(bk[:, s * 128:(s + 1) * 128], st[:, s, c * 128:(c + 1) * 128], self.IDF[:])
                    return last
                T.op("pe", [stb, self.IDFb] + subs, [bb], emit)
                if c % 2 == 0:
                    T.op("act", [bb], [self.Hb[c][t]], lambda c=c, t=t, bk=bk: nc.scalar.copy(
                        self.H[:, c, t * 512:(t + 1) * 512], bk[:]))
                else:
                    T.op("dve", [bb], [self.Hb[c][t]], lambda c=c, t=t, bk=bk: nc.vector.tensor_copy(
                        self.H[:, c, t * 512:(t + 1) * 512], bk[:]))
        if not sample or phase == "dma":
            return
        T.dma("sp", [], [self.XSSb], self.XSS[:], self.xs)
        for c in range(8):
            bk, bb = self.bank()
            T.op("pe", [self.XSSb, self.IDFb], [bb], lambda c=c, bk=bk: nc.tensor.transpose(
                bk[:, 0:16], self.XSS[:, c * 128:(c + 1) * 128], self.IDF[0:16, 0:16]))
            T.op("act", [bb], [self.Hb[c][4]], lambda c=c, bk=bk: nc.scalar.copy(self.H[:, c, SEQ:TP], bk[:, 0:16]))

    def rmsnorm(self, tiles, grow, final=False, presq=False):
        nc, T = self.nc, self.T
        rs = {}
        for t in tiles:
            bk, bb = self.bank()
            w = t.w
            if not presq and w <= 64:
                sq, sqb = self.ring("sq")
                for c in range(8):
                    T.op("act", [self.Hb[c][t.idx]], [sqb], lambda c=c, t=t, sq=sq, w=w: nc.scalar.activation(
                        sq[:, c * w:(c + 1) * w], self.H[:, c, t.h0:t.h0 + w], AF.Square))
                for c in range(8):
                    T.op("pe", [sqb, self.ONb], [bb], lambda c=c, bk=bk, sq=sq, w=w: nc.tensor.matmul(
                        bk[:, 0:w], self.ONB[:], sq[:, c * w:(c + 1) * w], start=(c == 0), stop=(c == 7)))
                r, rb = self.ring("rs")
                self.act_rsqrt(r[:, 0:w], rb, bk[:, 0:w], [bb], 1.0 / D)
                rs[t.idx] = (r, rb)
                continue
            for c in range(8):
                if presq == "xnow":
                    sq, sqb = self.SQX[:, c, 0:w], self.SQXb[c]
                    T.op("act", [self.Hb[c][t.idx]], [sqb], lambda c=c, t=t, sq=sq, w=w: nc.scalar.activation(
                        sq[:, 0:w], self.H[:, c, t.h0:t.h0 + w], AF.Square))
                elif presq == "x":
                    sq, sqb = self.SQX[:, c, 0:w], self.SQXb[c]
                elif presq:
                    sq, sqb = self.N[:, c, t.lo:t.lo + w], self.Nb[c][t.li]
                else:
                    sq, sqb = self.ring("sq")
                    T.op("act", [self.Hb[c][t.idx]], [sqb], lambda c=c, t=t, sq=sq, w=w: nc.scalar.activation(
                        sq[:, 0:w], self.H[:, c, t.h0:t.h0 + w], AF.Square))
                T.op("pe", [sqb, self.ONb], [bb], lambda c=c, bk=bk, sq=sq, w=w: nc.tensor.matmul(
                    bk[:, 0:w], self.ONB[:], sq[:, 0:w], start=(c == 0), stop=(c == 7)))
            r, rb = self.ring("rs")
            self.act_rsqrt(r[:, 0:w], rb, bk[:, 0:w], [bb], 1.0 / D)
            rs[t.idx] = (r, rb)
        for t in tiles:
            r, rb = rs[t.idx]
            w = t.w
            for c in range(8):
                if final:
                    T.op("dve", [self.Hb[c][t.idx], rb, self.PARAMb], [self.Hb[c][t.idx]],
                         lambda c=c, t=t, r=r, w=w: nc.vector.scalar_tensor_tensor(
                             self.H[:, c, t.h0:t.h0 + w], self.H[:, c, t.h0:t.h0 + w], self.prow(c, grow), r[:, 0:w],
                             op0=ALU.mult, op1=ALU.mult))
                else:
                    T.op("dve", [self.Hb[c][t.idx], rb, self.PARAMb], [self.Nb[c][t.li]],
                         lambda c=c, t=t, r=r, w=w: nc.vector.scalar_tensor_tensor(
                             self.N[:, c, t.lo:t.lo + w], self.H[:, c, t.h0:t.h0 + w], self.prow(c, grow), r[:, 0:w],
                             op0=ALU.mult, op1=ALU.mult))

    def p_dma(self, l, seg, tiles):
        T = self.T
        for t in tiles:
            if t.samp:
                T.dma("sp", [], [self.PSSb], self.PSS[:], self.psm[l])
            else:
                ps, psb = self.PSL[t.li]
                T.dma("sp", [], [psb], ps[:], self.pp[l, t.h0:t.h0 + 512, :].rearrange("(s p) d -> p s d", p=128))

    def p_transposes(self, l, seg, tiles):
        nc, T = self.nc, self.T
        for t in tiles:
            if t.samp:
                for cc in range(2):
                    bk, bb = self.bank()
                    T.op("pe", [self.PSSb, self.IDFb], [bb], lambda cc=cc, bk=bk: nc.tensor.transpose(
                        bk[:, 0:16], self.PSS[:, cc * 128:(cc + 1) * 128], self.IDF[0:16, 0:16]))
                    T.op("act", [bb], [self.PTb[2]], lambda cc=cc, bk=bk, t=t: nc.scalar.copy(
                        self.PT[:, cc, t.lo:t.lo + 16], bk[:, 0:16]))
            else:
                ps, psb = self.PSL[t.li]
                for cc in range(2):
                    bk, bb = self.bank()

                    def emit(cc=cc, bk=bk, ps=ps):
                        last = None
                        for s in range(4):
                            last = nc.tensor.transpose(bk[:, s * 128:(s + 1) * 128], ps[:, s, cc * 128:(cc + 1) * 128], self.IDF[:])
                        return last
                    T.op("pe", [psb, self.IDFb], [bb], emit)
                    T.op("act", [bb], [self.PTb[t.li]], lambda cc=cc, bk=bk, t=t: nc.scalar.copy(
                        self.PT[:, cc, t.lo:t.lo + 512], bk[:]))

    def build_diag(self, l, j, slot):
        nc, T = self.nc, self.T
        T.op("dve", [self.IDBb, self.CAWTb], [self.DIAGb[slot]], lambda: nc.vector.tensor_tensor(
            self.DIAG[slot][:], self.IDB[:].unsqueeze(1).broadcast_to([128, 31, 128]),
            self.CAWT[:, l, j, :].unsqueeze(2).broadcast_to([128, 31, 128]), op=ALU.mult))

    def sample_state_steps(self, l):
        nc, T = self.nc, self.T
        steps = []

        def stg(k):
            return self.SST2[k % 2]
        k = 0
        for g in range(4):
            def dma(k=k, g=g):
                v, b = stg(k)
                T.dma("sp", [], [b], v[0:120, :], self.sca[l, 4 * g:4 * g + 4].rearrange("b k c -> (b k) c"))

            def comp(k=k, g=g):
                v, b = stg(k)
                for j in range(3):
                    bk, bb = self.bank()
                    T.op("pe", [b, self.IDFb], [bb], lambda j=j, bk=bk: nc.tensor.transpose(
                        bk[:, 0:120], v[0:120, j * 128:(j + 1) * 128], self.IDF[0:120, 0:120]))
                    T.op("dve", [bb, self.CAWTb], [self.TMPSb], lambda j=j, bk=bk: nc.vector.tensor_tensor(
                        self.TMPS[:], bk[:, 0:120].rearrange("p (b k) -> p b k", b=4),
                        self.CAWT[:, l, j, 0:30].unsqueeze(1).broadcast_to([128, 4, 30]), op=ALU.mult))
                    T.op("dve", [self.TMPSb], [self.YASb], lambda j=j: nc.vector.tensor_reduce(
                        self.YAS[:, j, 4 * g:4 * g + 4], self.TMPS[:], axis=AX.X, op=ALU.add))
            steps.append((dma, comp)); k += 1
        for g in range(2):
            def dma(k=k, g=g):
                v, b = stg(k)
                T.dma("sp", [], [b], v[0:120, 0:DB], self.spl[l, 8 * g:8 * g + 8].rearrange("b k c -> (b k) c"))

            def comp(k=k, g=g):
                v, b = stg(k)
                for i in range(2):
                    bk, bb = self.bank()
                    T.op("pe", [b, self.IDFb], [bb], lambda i=i, bk=bk: nc.tensor.transpose(
                        bk[:, 0:120], v[0:120, i * 128:(i + 1) * 128], self.IDF[0:120, 0:120]))
                    for h in range(2):
                        wn = WIN[2 * i + h]
                        T.op("dve", [bb], [self.SSSb], lambda i=i, h=h, wn=wn, bk=bk: nc.vector.tensor_reduce(
                            self.SSS[h * 64:(h + 1) * 64, i, 8 * g:8 * g + 8],
                            bk[h * 64:(h + 1) * 64, 0:120].rearrange("p (b k) -> p b k", b=8)[:, :, 15 - (wn - 1):15],
                            axis=AX.X, op=ALU.add))
            steps.append((dma, comp)); k += 1

        def dma(k=k):
            v, b = stg(k)
            T.dma("sp", [], [b], v[0:32, :], self.scc[l].rearrange("b k c -> (b k) c"))

        def comp(k=k):
            v, b = stg(k)
            for j in range(3):
                bk, bb = self.bank()
                T.op("pe", [b, self.IDFb], [bb], lambda j=j, bk=bk: nc.tensor.transpose(
                    bk[:, 0:32], v[0:32, j * 128:(j + 1) * 128], self.IDF[0:32, 0:32]))
                T.op("act", [bb], [self.SCTb], lambda j=j, bk=bk: nc.scalar.copy(self.SCT[:, j, :], bk[:, 0:32]))
        steps.append((dma, comp))
        return steps

    def state_shift_copies(self):
        T = self.T
        for l in range(DEPTH):
            self.out_evs.append(T.dma("sp", [], [], self.nas[l][:, 0:29, :], self.sca[l][:, 1:30, :]))
            self.out_evs.append(T.dma("sp", [], [], self.nps[l][:, 0:14, :], self.spl[l][:, 1:15, :]))
            self.out_evs.append(T.dma("sp", [], [], self.ncs[l][:, 0:1, :], self.scc[l][:, 1:2, :]))

    def flush_deferred(self):
        d, self.deferred = self.deferred, []
        lo, self.deferred_lo = self.deferred_lo, []
        ds = [e for e in d if e[1].samp]
        dn = [e for e in d if not e[1].samp]
        for f, t in ds:
            f(t)
        for f, t in dn[:1]:
            f(t)
        for f, t in lo:
            f(t)
        for f, t in dn[1:]:
            f(t)

    def run_phase(self, specs, tiles, do, post_tile=None, s_tail=0, r_head=0, post_delay=0, defer=True, flush_first=False):
        M = len(specs)
        rest = tiles[1:]

        def acq(m):
            return [self.w_acquire(sp) for sp in specs[m]]
        if flush_first:
            self.flush_deferred()
        if r_head > 0:
            sl = {i: acq(i) for i in range(r_head)}
            for i in range(r_head):
                do(i, sl[i], tiles[0])
                if i == 0:
                    self.flush_deferred()
            for i in range(r_head):
                for t in rest:
                    do(i, sl[i], t)
            self.w_release(sum(len(specs[i]) for i in range(r_head)))
        self.flush_deferred()
        for m in range(r_head, M - s_tail):
            sl1 = acq(m)
            for t in tiles:
                do(m, sl1, t)
            self.w_release(len(specs[m]))
        tail = list(range(M - s_tail, M))
        sl = {i: acq(i) for i in tail}
        for i in tail:
            do(i, sl[i], tiles[0])
        pending = [tiles[0]] if post_tile is not None else []
        for ti, t in enumerate(rest):
            for n_, i in enumerate(tail):
                if pending and (ti > 0 or n_ >= post_delay):
                    post_tile(pending.pop())
                do(i, sl[i], t)
            if pending:
                post_tile(pending.pop())
            if post_tile is not None:
                if defer:
                    self.deferred.append((post_tile, t))
                else:
                    post_tile(t)
        if pending:
            post_tile(pending.pop())
        if tail:
            self.w_release(sum(len(specs[i]) for i in tail))

    def p1(self, l, seg, post_tile):
        nc, T = self.nc, self.T
        tiles = SEG_TILES[seg]
        ptiles = [t for t in tiles if not t.samp]
        stile = tiles[2] if seg == 1 else None
        self.p_dma(l, seg, tiles)
        for j in range(3):
            if seg == 0:
                T.op("dve", [], [self.GPXb[j]], lambda j=j: nc.vector.memset(self.GBS[:, j, 0:30], 0.0))
            else:
                T.op("dve", [self.GCb], [self.GPXb[j]], lambda j=j: nc.vector.tensor_copy(self.GBS[:, j, 0:30], self.GC[:, j, :]))

        def a_proj(j, sl, t):
            sg, sv = sl
            w = t.w
            bg, bgb = self.mm(sg, t, perk=(j == 0))
            bv, bvb = self.mm(sv, t)
            sig, sigb = self.ring("sig")
            self.act_sigmoid(sig[:, 0:w], sigb, bg[:, 0:w], [bgb])
            if not t.samp:
                T.op("dve", [bvb, sigb], [self.GPb[j][t.li]], lambda: nc.vector.tensor_tensor(
                    self.GBS[:, j, 30 + t.lo:30 + t.lo + 512], bv[:], sig[:], op=ALU.mult))
                if seg == 1 and t.li == 1:
                    T.op("dve", [bvb, sigb], [self.STAb], lambda: nc.vector.tensor_tensor(
                        self.STA[:, j, 0:30], bv[:, 482:512], sig[:, 482:512], op=ALU.mult))
            else:
                T.op("dve", [bvb, sigb], [self.STAb], lambda: nc.vector.tensor_tensor(
                    self.STA[:, j, 30:46], bv[:, 0:16], sig[:, 0:16], op=ALU.mult))
        self.run_phase([[("w_in", l, 0, 8, (3 + j) * 128), ("w_in", l, 0, 8, j * 128)] for j in range(3)],
                       tiles, a_proj, r_head=2)
        if seg == 0:
            for j in range(3):
                T.op("dve", [self.GPb[j][1]], [self.GCb], lambda j=j: nc.vector.tensor_copy(
                    self.GC[:, j, :], self.GBS[:, j, 1024:1054]))
        self.p_transposes(l, seg, tiles)
        if l == 0 and seg == 0:
            for j in range(3):
                self.build_diag(0, j, j)
        for i in range(2):
            if seg == 0:
                T.op("dve", [], [self.USb], lambda: nc.vector.memset(self.US[:, 0:15], 0.0))
            else:
                T.op("dve", [self.UCb], [self.USb], lambda i=i: nc.vector.tensor_copy(self.US[:, 0:15], self.UC[:, i, :]))
            su = self.w_acquire(("w_in", l, 0, 8, (6 + i) * 128))
            for t in tiles:
                bu, bub = self.mm(su, t)
                if not t.samp:
                    T.op("act", [bub], [self.USb], lambda bu=bu, t=t: nc.scalar.copy(self.US[:, 15 + t.lo:15 + t.lo + 512], bu[:]))
                else:
                    T.op("act", [bub], [self.STUb], lambda bu=bu, i=i: nc.scalar.copy(self.STU[:, i, 15:31], bu[:, 0:16]))
            self.w_release(1)
            if seg == 0:
                T.op("dve", [self.USb], [self.UCb], lambda i=i: nc.vector.tensor_copy(self.UC[:, i, :], self.US[:, 1024:1039]))
            else:
                T.op("dve", [self.USb], [self.STUb], lambda i=i: nc.vector.tensor_copy(self.STU[:, i, 0:15], self.US[:, 1024:1039]))
            for t in ptiles:
                U = self.US[:, t.lo:t.lo + 527]
                T2, T4, T8, T16 = self.TT
                T.op("dve", [self.USb], [self.TTb[0]], lambda U=U: nc.vector.tensor_tensor(
                    T2[:, 1:527], U[:, 1:527], U[:, 0:526], op=ALU.add))
                T.op("dve", [self.TTb[0]], [self.TTb[1]], lambda: nc.vector.tensor_tensor(
                    T4[:, 3:527], T2[:, 3:527], T2[:, 1:525], op=ALU.add))
                if i == 0:
                    srcs = [(T2, self.TTb[0]), (T4, self.TTb[1])]
                else:
                    T.op("dve", [self.TTb[1]], [self.TTb[2]], lambda: nc.vector.tensor_tensor(
                        T8[:, 7:527], T4[:, 7:527], T4[:, 3:523], op=ALU.add))
                    T.op("dve", [self.TTb[2]], [self.TTb[3]], lambda: nc.vector.tensor_tensor(
                        T16[64:128, 15:527], T8[64:128, 15:527], T8[64:128, 7:519], op=ALU.add))
                    srcs = [(T8, self.TTb[2]), (T16, self.TTb[3])]
                for h in range(2):
                    S, Sb = srcs[h]
                    pr = slice(h * 64, (h + 1) * 64)
                    T.op("dve", [Sb, self.USb, self.INVb], [self.PLDb[t.li]], lambda S=S, U=U, pr=pr, i=i, t=t: nc.vector.scalar_tensor_tensor(
                        self.PLD[t.li][pr, i, :], S[pr, 15:527], self.INVW[pr, i:i + 1], U[pr, 15:527],
                        op0=ALU.mult, op1=ALU.subtract))
                    if t.idx == 0:
                        T.op("dve", [Sb, self.INVb], [self.SMTb], lambda S=S, pr=pr, i=i: nc.vector.tensor_tensor(
                            self.SMT[pr, 0, 0:15], S[pr, 15:30], self.INVC[pr, i, 0:15], op=ALU.mult))
                        T.op("dve", [self.SMTb, self.USb], [self.PLDb[t.li]], lambda U=U, pr=pr, i=i, t=t: nc.vector.tensor_tensor(
                            self.PLD[t.li][pr, i, 0:15], self.SMT[pr, 0, 0:15], U[pr, 15:30], op=ALU.subtract))
            if stile is not None:
                T.op("dve", [self.SSSb, self.STUb], [self.SMTb], lambda i=i: nc.vector.tensor_tensor(
                    self.SMT[:, 1, :], self.SSS[:, i, :], self.STU[:, i, 15:31], op=ALU.add))
                T.op("dve", [self.SMTb, self.STUb, self.INVb], [self.PLSb], lambda i=i: nc.vector.scalar_tensor_tensor(
                    self.PLS[:, i, :], self.SMT[:, 1, :], self.INVW[:, i:i + 1], self.STU[:, i, 15:31],
                    op0=ALU.mult, op1=ALU.subtract))
        def conv(j, t, ds):
            bk, bb = self.bank()
            rb = [self.GPb[j][0], self.GPXb[j]] if t.li == 0 else [self.GPb[j][0], self.GPb[j][1]]

            def emit():
                last = None
                for k in range(31):
                    last = nc.tensor.matmul(bk[:], self.DIAG[ds][:, k, :], self.GBS[:, j, t.lo + k:t.lo + k + 512],
                                            start=(k == 0), stop=(k == 30))
                return last
            T.op("pe", [self.DIAGb[ds]] + rb, [bb], emit)
            T.op("act", [bb, self.PARAMb], [self.YAb[t.li]], lambda: nc.scalar.activation(
                self.YA[t.li][:, j, :], bk[:], AF.Identity, bias=self.prow(j, R_CAB + l)))
            T.op("act", [bb, self.PARAMb], [self.YSb[t.li]], lambda: nc.scalar.activation(
                self.YS[t.li][:, j, :], bk[:], AF.Square, bias=self.prow(j, R_CAB + l)))

        def layer_norm(t):
            w = t.w
            if t.samp:
                ya, yab, ysq, ysb = self.YAS, self.YASb, self.YSS, self.YSSb
            else:
                ya, yab, ysq, ysb = self.YA[t.li], self.YAb[t.li], self.YS[t.li], self.YSb[t.li]
            bs, bsb = self.bank()

            def emit_s():
                last = None
                for j in range(3):
                    last = nc.tensor.matmul(bs[:, 0:w], self.ONF[:], ya[:, j, 0:w], start=(j == 0), stop=(j == 2))
                return last
            T.op("pe", [yab, self.ONb], [bsb], emit_s)
            bq, bqb = self.bank()

            def emit_q():
                last = None
                for j in range(3):
                    last = nc.tensor.matmul(bq[:, 0:w], self.ONB[:], ysq[:, j, 0:w], start=(j == 0), stop=(j == 2))
                return last
            T.op("pe", [ysb, self.ONb], [bqb], emit_q)
            if t.samp:
                mean, meanb = self.LNS[:, 0, :], self.LNSb[0]
                rstd, rstdb = self.LNS[:, 1, :], self.LNSb[1]
            else:
                mean, meanb = self.ring("rs")
                rstd, rstdb = self.ring("rs")
            T.op("act", [bsb], [meanb], lambda: nc.scalar.activation(mean[:, 0:w], bs[:, 0:w], AF.Copy, scale=1.0 / DA))
            T.op("dve", [meanb], [rstdb], lambda: nc.vector.scalar_tensor_tensor(
                rstd[:, 0:w], mean[:, 0:w], -1.0, mean[:, 0:w], op0=ALU.mult, op1=ALU.mult))
            T.op("dve", [bqb, rstdb], [rstdb], lambda: nc.vector.scalar_tensor_tensor(
                rstd[:, 0:w], bq[:, 0:w], 1.0 / DA, rstd[:, 0:w], op0=ALU.mult, op1=ALU.add))
            T.op("dve", [rstdb], [rstdb], lambda: nc.vector.tensor_scalar(
                rstd[:, 0:w], rstd[:, 0:w], 0.0, None, op0=ALU.max))
            self.act_rsqrt(rstd[:, 0:w], rstdb, rstd[:, 0:w], [rstdb], 1.0)
            yield
            T.op("dve", [yab, meanb], [yab], lambda: nc.vector.tensor_tensor(
                ya[:, :, 0:w], ya[:, :, 0:w], mean[:, 0:w].unsqueeze(1).broadcast_to([128, 3, w]), op=ALU.subtract))
            T.op("dve", [yab, rstdb], [yab], lambda: nc.vector.tensor_tensor(
                ya[:, :, 0:w], ya[:, :, 0:w], rstd[:, 0:w].unsqueeze(1).broadcast_to([128, 3, w]), op=ALU.mult))
            yield
            for j in range(3):
                sig, sigb = self.ring("sig")
                self.act_sigmoid(sig[:, 0:w], sigb, ya[:, j, 0:w], [yab],
                                 nscale=self.nprow(j, R_LNG + l), nbias=self.nprow(j, R_LNB + l))
                T.op("dve", [yab, self.PARAMb], [yab], lambda j=j: nc.vector.tensor_scalar(
                    ya[:, j, 0:w], ya[:, j, 0:w], self.prow(j, R_LNG + l), self.prow(j, R_LNB + l), op0=ALU.mult, op1=ALU.add))
                T.op("dve", [yab, sigb], [self.MIXb[j][t.li]], lambda sig=sig, j=j: nc.vector.tensor_tensor(
                    self.MIX[:, j, t.lo:t.lo + w], ya[:, j, 0:w], sig[:, 0:w], op=ALU.mult))
                yield

        live = []

        def pump(n=1):
            for _ in range(n):
                for g in list(live):
                    try:
                        next(g)
                    except StopIteration:
                        live.remove(g)

        tA, tB = ptiles
        for j in range(3):
            conv(j, tA, j)
        conv(0, tB, 0)
        live.append(layer_norm(tA))
        pump(1)
        conv(1, tB, 1)
        pump(2)
        conv(2, tB, 2)
        pump(3)
        live.append(layer_norm(tB))
        if stile is not None:
            for j in range(3):
                T.op("dve", [self.STAb, self.CAWTb, self.YASb], [self.YASb], lambda j=j: nc.vector.scalar_tensor_tensor(
                    self.YAS[:, j, :], self.STA[:, j, 30:46], self.CAWT[:, l, j, 30:31], self.YAS[:, j, :],
                    op0=ALU.mult, op1=ALU.add))
                T.op("dve", [self.YASb, self.PARAMb], [self.YASb], lambda j=j: nc.vector.tensor_scalar(
                    self.YAS[:, j, :], self.YAS[:, j, :], self.prow(j, R_CAB + l), None, op0=ALU.add))
                T.op("dve", [self.YASb], [self.YSSb], lambda j=j: nc.vector.tensor_tensor(
                    self.YSS[:, j, :], self.YAS[:, j, :], self.YAS[:, j, :], op=ALU.mult))
            live.append(layer_norm(stile))
        for i in range(2):
            for t in ptiles:
                bk, bb = self.bank()
                T.op("pe", [self.PLDb[t.li], self.PWb], [bb], lambda bk=bk, i=i, t=t: nc.tensor.matmul(
                    bk[:], self.PW[:, l, i, :], self.PLD[t.li][:, i, :], start=True, stop=True))
                T.op("act", [bb, self.PARAMb], [self.MIXb[3 + i][t.li]], lambda bk=bk, i=i, t=t: nc.scalar.activation(
                    self.MIX[:, 3 + i, t.lo:t.lo + 512], bk[:], AF.Copy, scale=self.prow(i, R_PSC + l)))
            if stile is not None:
                t = stile
                bk, bb = self.bank()
                T.op("pe", [self.PLSb, self.PWb], [bb], lambda bk=bk, i=i: nc.tensor.matmul(
                    bk[:, 0:16], self.PW[:, l, i, :], self.PLS[:, i, :], start=True, stop=True))
                T.op("act", [bb, self.PARAMb], [self.MIXb[3 + i][t.li]], lambda bk=bk, i=i, t=t: nc.scalar.activation(
                    self.MIX[:, 3 + i, t.lo:t.lo + 16], bk[:, 0:16], AF.Copy, scale=self.prow(i, R_PSC + l)))

        def c_proj(j):
            if seg == 0:
                T.op("dve", [], [self.VSb], lambda: nc.vector.memset(self.VS[:, 0:2], 0.0))
            else:
                T.op("dve", [self.VCb], [self.VSb], lambda j=j: nc.vector.tensor_copy(self.VS[:, 0:2], self.VC[:, j, :]))
            sc_ = self.w_acquire(("w_in", l, 0, 8, (11 + j) * 128))
            sx = self.w_acquire(("w_in", l, 0, 8, (14 + j) * 128))
            sbb = self.w_acquire(("w_in", l, 0, 8, (8 + j) * 128))
            w0, w1, w2 = (self.prow(j, R_CCW + l * 3 + k) for k in range(3))
            for t in tiles:
                w = t.w
                bc, bcb = self.mm(sc_, t)
                bx, bxb = self.mm(sx, t)
                bbk, bbb = self.mm(sbb, t)
                cc, ccb = self.ring("sig")
                T.op("act", [bcb], [ccb], lambda bc=bc, cc=cc, w=w: nc.scalar.copy(cc[:, 0:w], bc[:, 0:w]))
                if not t.samp:
                    V = self.VS[:, t.lo:t.lo + 514]
                    T.op("dve", [bxb, ccb], [self.VSb], lambda bx=bx, cc=cc, V=V: nc.vector.tensor_tensor(
                        V[:, 2:514], bx[:], cc[:], op=ALU.mult))
                    tm, tmb = self.ring("sig")
                    T.op("act", [self.VSb, self.PARAMb], [tmb], lambda V=V, tm=tm: nc.scalar.activation(
                        tm[:], V[:, 2:514], AF.Copy, scale=w2))
                    T.op("dve", [self.VSb, self.PARAMb, tmb], [tmb], lambda V=V, tm=tm: nc.vector.scalar_tensor_tensor(
                        tm[:], V[:, 1:513], w1, tm[:], op0=ALU.mult, op1=ALU.add))
                    T.op("dve", [self.VSb, self.PARAMb, tmb], [tmb], lambda V=V, tm=tm: nc.vector.scalar_tensor_tensor(
                        tm[:], V[:, 0:512], w0, tm[:], op0=ALU.mult, op1=ALU.add))
                    T.op("dve", [bbb, tmb], [self.MIXb[5 + j][t.li]], lambda bbk=bbk, tm=tm, j=j, t=t: nc.vector.tensor_tensor(
                        self.MIX[:, 5 + j, t.lo:t.lo + 512], bbk[:], tm[:], op=ALU.mult))
                else:
                    T.op("dve", [bxb, ccb], [self.STVb], lambda bx=bx, cc=cc, j=j: nc.vector.tensor_tensor(
                        self.STV[:, j, 2:18], bx[:, 0:16], cc[:, 0:16], op=ALU.mult))
                    sct = self.SCT[:, j, :].rearrange("p (b k) -> p b k", k=2)
                    T.op("dve", [self.STVb, self.PARAMb], [self.SMTb], lambda j=j: nc.vector.tensor_scalar(
                        self.SMT[:, 2, :], self.STV[:, j, 2:18], w2, None, op0=ALU.mult))
                    T.op("dve", [self.SCTb, self.PARAMb, self.SMTb], [self.SMTb], lambda sct=sct: nc.vector.scalar_tensor_tensor(
                        self.SMT[:, 2, :], sct[:, :, 1], w1, self.SMT[:, 2, :], op0=ALU.mult, op1=ALU.add))
                    T.op("dve", [self.SCTb, self.PARAMb, self.SMTb], [self.SMTb], lambda sct=sct: nc.vector.scalar_tensor_tensor(
                        self.SMT[:, 2, :], sct[:, :, 0], w0, self.SMT[:, 2, :], op0=ALU.mult, op1=ALU.add))
                    T.op("dve", [bbb, self.SMTb], [self.MIXb[5 + j][t.li]], lambda bbk=bbk, j=j, t=t: nc.vector.tensor_tensor(
                        self.MIX[:, 5 + j, t.lo:t.lo + 16], bbk[:, 0:16], self.SMT[:, 2, :], op=ALU.mult))
                pump(2 if j == 0 else 1)
            self.w_release(3)
            if seg == 0:
                T.op("dve", [self.VSb], [self.VCb], lambda j=j: nc.vector.tensor_copy(self.VC[:, j, :], self.VS[:, 1024:1026]))
            else:
                T.op("dve", [self.VSb], [self.STVb], lambda j=j: nc.vector.tensor_copy(self.STV[:, j, 0:2], self.VS[:, 1024:1026]))

        c_proj(0)
        c_proj(1)
        c_proj(2)
        pump(8)
        if seg == 1:
            self.state_outputs(l)
        korder = [3, 4, 5, 6, 0, 1, 2, 7]

        def o_proj(m, sl, t):
            bk, bb = self.mm(sl[0], t, rhs=lambda k: self.MIX[:, k, t.lo:t.lo + t.w],
                             rbufs=[self.MIXb[c][t.li] for c in range(8)], korder=korder, perk=(m == 0))
            T.op("dve", [bb, self.Hb[m][t.idx]], [self.Hb[m][t.idx]], lambda: nc.vector.tensor_tensor(
                self.H[:, m, t.h0:t.h0 + t.w], self.H[:, m, t.h0:t.h0 + t.w], bk[:, 0:t.w], op=ALU.add))
            T.op("act", [self.Hb[m][t.idx]], [self.Nb[m][t.li]], lambda: nc.scalar.activation(
                self.N[:, m, t.lo:t.lo + t.w], self.H[:, m, t.h0:t.h0 + t.w], AF.Square))
        self.run_phase([[("w_out", l, 0, 8, m * 128)] for m in range(8)], tiles, o_proj, post_tile=post_tile, s_tail=4, post_delay=2)

    def state_outputs(self, l):
        nc, T = self.nc, self.T
        specs = [(self.STA, self.STAb, 3, 46, 30, self.nap, self.nas, 29),
                 (self.STU, self.STUb, 2, 31, 15, self.npp, self.nps, 14),
                 (self.STV, self.STVb, 3, 18, 2, self.ncp, self.ncs, 1)]
        for si, (st, stb, nch, n, npr, outp, outs, srow) in enumerate(specs):
            ost, ostb = self.OSTL[si]
            for j in range(nch):
                bk, bb = self.bank()
                T.op("pe", [stb, self.IDFb], [bb], lambda st=st, j=j, n=n, bk=bk: nc.tensor.transpose(
                    bk[0:n, 0:128], st[:, j, 0:n], self.IDF[:]))
                T.op("act", [bb], [ostb], lambda j=j, n=n, bk=bk, ost=ost: nc.scalar.copy(ost[0:n, j * 128:(j + 1) * 128], bk[0:n, 0:128]))
            wd = nch * 128
            self.out_evs.append(T.dma("sp", [ostb], [], outp[l], ost[0:npr, 0:wd]))
            self.out_evs.append(T.dma("sp", [ostb], [], outs[l][:, srow, :], ost[npr:n, 0:wd]))

    def p2(self, l, seg, post_tile, bg=None):
        nc, T = self.nc, self.T
        tiles = SEG_TILES[seg]

        def up(m, sl, t):
            w = t.w
            bk, bb = self.mm(sl[0], t, perk=(m == 0))
            tm, tmb = self.ring("rl")
            hb = self.HIDb[m][t.li]
            if m % 2 == 0:
                T.op("act", [bb], [tmb], lambda: nc.scalar.activation(tm[:, 0:w], bk[:, 0:w], AF.Relu))
                T.op("dve", [tmb], [hb], lambda: nc.vector.tensor_tensor(
                    self.HID[:, m, t.lo:t.lo + w], tm[:, 0:w], tm[:, 0:w], op=ALU.mult))
            else:
                T.op("dve", [bb], [tmb], lambda: nc.vector.tensor_scalar(
                    tm[:, 0:w], bk[:, 0:w], 0.0, None, op0=ALU.max))
                T.op("act", [tmb], [hb], lambda: nc.scalar.activation(
                    self.HID[:, m, t.lo:t.lo + w], tm[:, 0:w], AF.Square))
            if bg is not None and t.li == 0 and m in bg:
                bg[m]()
        self.run_phase([[("w_up", l, 0, 8, m * 128)] for m in range(32)], tiles, up, r_head=4)

        def down(m, sl, t):
            bk, bb = self.bank()
            w = t.w

            def emit():
                last = None
                for k in range(32):
                    last = nc.tensor.matmul(bk[:, 0:w], self.WS[sl[k // 8]][:, k % 8, :], self.HID[:, k, t.lo:t.lo + w],
                                            start=(k == 0), stop=(k == 31))
                return last
            T.op("pe", [self.WSb[q] for q in sl] + [self.HIDb[k][t.li] for k in range(32)], [bb], emit)
            T.op("dve", [bb, self.Hb[m][t.idx]], [self.Hb[m][t.idx]], lambda: nc.vector.tensor_tensor(
                self.H[:, m, t.h0:t.h0 + w], self.H[:, m, t.h0:t.h0 + w], bk[:, 0:w], op=ALU.add))
            T.op("act", [self.Hb[m][t.idx]], [self.Nb[m][t.li]], lambda: nc.scalar.activation(
                self.N[:, m, t.lo:t.lo + w], self.H[:, m, t.h0:t.h0 + w], AF.Square))
        self.run_phase([[("w_down", l, kq * 1024, 8, m * 128) for kq in range(4)] for m in range(8)], tiles, down,
                       post_tile=post_tile, s_tail=1)

    def pre_squares(self, pt):
        nc, T = self.nc, self.T
        for c in range(8):
            T.op("act", [self.Hb[c][pt.idx]], [self.SQXb[c]], lambda c=c: nc.scalar.activation(
                self.SQX[:, c, 0:pt.w], self.H[:, c, pt.h0:pt.h0 + pt.w], AF.Square))

    def p3(self, l, seg, post_tile, hooks=None, pre_sq_tile=None, last_sq=False):
        nc, T = self.nc, self.T
        tiles = SEG_TILES[seg]
        if hooks is not None and -1 in hooks:
            for f in hooks[-1]:
                f()
        if pre_sq_tile is not None:
            self.pre_squares(pre_sq_tile)

        def ple(m, sl, t):
            w = t.w
            bg_, bgb = self.mm(sl[0], t, perk=(m == 0))
            bp, bpb = self.mm(sl[1], t, nk=2, rhs=lambda k: self.PT[:, k, t.lo:t.lo + t.w], rbufs=[self.PTb[t.li]])
            g, gb = self.ring("rl")
            T.op("act", [bgb], [gb], lambda: nc.scalar.activation(g[:, 0:w], bg_[:, 0:w], AF.Sigmoid))
            T.op("dve", [bpb, gb], [gb], lambda: nc.vector.tensor_tensor(g[:, 0:w], bp[:, 0:w], g[:, 0:w], op=ALU.mult))
            T.op("dve", [gb, self.Hb[m][t.idx]], [self.Hb[m][t.idx]], lambda: nc.vector.tensor_tensor(
                self.H[:, m, t.h0:t.h0 + w], self.H[:, m, t.h0:t.h0 + w], g[:, 0:w], op=ALU.add))
            if last_sq and t.li == 1:
                T.op("act", [self.Hb[m][t.idx]], [self.SQXb[m]], lambda: nc.scalar.activation(
                    self.SQX[:, m, 0:w], self.H[:, m, t.h0:t.h0 + w], AF.Square))
            if hooks is not None and t.li == 0 and m in hooks:
                for f in hooks[m]:
                    f()
        self.run_phase([[("w_pg", l, 0, 8, m * 128), ("w_pp", l, 0, 2, m * 128)] for m in range(8)], tiles, ple,
                       post_tile=post_tile, s_tail=3, r_head=3, post_delay=1, flush_first=True)

    def final_tile(self, t, presq_x=False):
        nc, T = self.nc, self.T
        if t.samp:
            mode = False
        elif presq_x:
            mode = "x" if t.li == 1 else False
        else:
            mode = "xnow"
        self.rmsnorm([t], R_GFIN, final=True, presq=mode)
        if t.samp:
            for c in range(8):
                bk, bb = self.bank()
                T.op("pe", [self.Hb[c][t.idx], self.IDFb], [bb], lambda c=c, bk=bk: nc.tensor.transpose(
                    bk[0:16, 0:128], self.H[:, c, SEQ:TP], self.IDF[:]))
                T.op("act", [bb], [self.XSSb], lambda c=c, bk=bk: nc.scalar.copy(self.XSS[:, c * 128:(c + 1) * 128], bk[0:16, 0:128]))
            self.out_evs.append(T.dma("sp", [self.XSSb], [], self.ys, self.XSS[:]))
            return
        self.deferred_lo.append((self.final_store, t))

    def final_store(self, t):
        nc, T = self.nc, self.T
        for s in range(4):
            q = self.ystnext
            self.ystnext = (q + 1) % 4
            st, stb = self.XST[0][:, q, :], self.YSTb[q]
            for cg in range(2):
                bk, bb = self.bank()

                def emit(bk=bk, s=s, cg=cg):
                    last = None
                    for cc in range(4):
                        c = cg * 4 + cc
                        last = nc.tensor.transpose(bk[:, cc * 128:(cc + 1) * 128],
                                                   self.H[:, c, t.h0 + s * 128:t.h0 + (s + 1) * 128], self.IDF[:])
                    return last
                T.op("pe", [self.Hb[cg * 4 + cc][t.idx] for cc in range(4)] + [self.IDFb], [bb], emit)
                if (s + cg) % 2 == 0:
                    T.op("act", [bb], [stb], lambda bk=bk, cg=cg, st=st: nc.scalar.copy(st[:, cg * 512:(cg + 1) * 512], bk[:]))
                else:
                    T.op("dve", [bb], [stb], lambda bk=bk, cg=cg, st=st: nc.vector.tensor_copy(st[:, cg * 512:(cg + 1) * 512], bk[:]))
            self.out_evs.append(T.dma("sp", [stb], [], self.yp[t.h0 + s * 128:t.h0 + (s + 1) * 128, :], st))

    def program(self):
        self.build_identity()
        if not self.T.dry:
            self.w_pump()
        self.prologue()
        order = [(l, seg) for l in range(DEPTH) for seg in range(2)]
        self.state_shift_copies()
        for n, (l, seg) in enumerate(order):
            nxt = order[n + 1] if n + 1 < len(order) else None
            self.p1(l, seg, post_tile=lambda t, l=l: self.rmsnorm([t], R_GMLP + l, presq=True))
            bg = None
            if nxt is not None and nxt[0] != l:
                bg = {16 + 4 * j: (lambda nl=nxt[0], j=j: self.build_diag(nl, j, j)) for j in range(3)}
            self.p2(l, seg, post_tile=lambda t, l=l: self.rmsnorm([t], R_GPLE + l, presq=True), bg=bg)

            def after_p3(t, l=l, seg=seg, nxt=nxt):
                if nxt is not None:
                    nl, ns = nxt
                    nts = SEG_TILES[ns]
                    if t.li < 2:
                        self.rmsnorm([nts[t.li]], R_GMIX + nl, presq=("x" if t.li == 0 else "xnow"))
                        if t.li == 1 and len(nts) == 3 and len(SEG_TILES[seg]) == 2:
                            self.rmsnorm([nts[2]], R_GMIX + nl)
                    elif len(nts) == 3:
                        self.rmsnorm([nts[2]], R_GMIX + nl)
                if l == DEPTH - 1:
                    self.final_tile(t, presq_x=(nxt is None))
            hooks = None
            if nxt is not None and nxt[1] == 1:
                st = self.sample_state_steps(nxt[0])
                hooks = {}
                hooks.setdefault(-1, []).extend([st[0][0], st[1][0]])
                for k in range(len(st)):
                    hooks.setdefault(k, []).append(st[k][1])
                    if k + 2 < len(st):
                        hooks.setdefault(k, []).append(st[k + 2][0])
            pre_sq_tile = SEG_TILES[nxt[1]][0] if nxt is not None else None
            if n == 0:
                pst = pre_sq_tile
                pre_sq_tile = None
                hooks.setdefault(-1, []).append(lambda: self.load_x(range(2, 4), phase="dma"))
                hooks.setdefault(3, []).append(lambda: self.load_x(range(2, 3), phase="tr"))
                hooks.setdefault(3, []).append(lambda pst=pst: self.pre_squares(pst))
                hooks.setdefault(5, []).append(lambda: self.load_x(range(3, 4), phase="tr"))
                hooks.setdefault(6, []).append(lambda: self.load_x(range(0), sample=True, phase="tr"))
            self.p3(l, seg, post_tile=after_p3, hooks=hooks, pre_sq_tile=pre_sq_tile, last_sq=(nxt is None))
        while self.deferred or self.deferred_lo:
            self.flush_deferred()
        if not self.T.dry:
            self.T._wait("sp", [e for e in self.out_evs if e is not None])

    def build(self):
        self.declare()
        with self.es:
            self.alloc()
            self.T.dry = True
            self.program()
            self.pbnext = self.signext = self.rsnext = self.rlnext = self.sqnext = self.ystnext = 0
            self.out_evs = []
            self.T.dry = False
            self.program()
        return self.nc


_NC_CACHE = {}


def _get_nc():
    if "nc" not in _NC_CACHE:
        _NC_CACHE["nc"] = Builder().build()
    return _NC_CACHE["nc"]


def kernel(x_prompt, x_sample, state_conv_a, state_pool, state_conv_c, p_prompt, p_sample,
           norm_mix_g, w_in, conv_a_w, conv_a_b, ln_a_g, ln_a_b, pool_w, pool_scale, conv_c_w,
           w_out, norm_mlp_g, w_up, w_down, norm_ple_g, w_ple_gate, w_ple_proj, final_norm_g):
    f = lambda a: np.ascontiguousarray(np.asarray(a, dtype=np.float32))
    x_prompt, x_sample = f(x_prompt), f(x_sample)
    state_conv_a, state_pool, state_conv_c = f(state_conv_a), f(state_pool), f(state_conv_c)
    p_prompt, p_sample = f(p_prompt), f(p_sample)
    shared = {
        "g_mix": f(norm_mix_g), "g_mlp": f(norm_mlp_g), "g_ple": f(norm_ple_g), "g_fin": f(final_norm_g).reshape(1, D),
        "caw": f(conv_a_w), "cab": f(conv_a_b), "lng": f(ln_a_g), "lnb": f(ln_a_b),
        "plw": f(pool_w), "psc": f(pool_scale), "ccw": f(conv_c_w),
        "w_in": f(w_in), "w_out": f(w_out), "w_up": f(w_up), "w_down": f(w_down),
        "w_pg": f(w_ple_gate), "w_pp": f(w_ple_proj),
    }
    in_maps = []
    for c in range(NCORES):
        sl = slice(c * NS_TOK, (c + 1) * NS_TOK)
        m = dict(shared)
        m["xp"] = f(x_prompt[c]); m["xs"] = f(x_sample[sl, 0, :])
        m["sca"] = f(state_conv_a[:, sl]); m["spl"] = f(state_pool[:, sl]); m["scc"] = f(state_conv_c[:, sl])
        m["pp"] = f(p_prompt[:, c]); m["psm"] = f(p_sample[:, sl, 0, :])
        in_maps.append(m)
    nc = _get_nc()
    res = run_bass_kernel_spmd(nc, in_maps, core_ids=list(range(NCORES)))
    rs = res.results
    y_prompt = np.stack([rs[c]["yp"] for c in range(NCORES)], axis=0)
    y_sample = np.concatenate([rs[c]["ys"] for c in range(NCORES)], axis=0)[:, None, :]
    nap = np.stack([rs[c]["nap"] for c in range(NCORES)], axis=1)
    npp = np.stack([rs[c]["npp"] for c in range(NCORES)], axis=1)
    ncp = np.stack([rs[c]["ncp"] for c in range(NCORES)], axis=1)
    nas = np.concatenate([rs[c]["nas"] for c in range(NCORES)], axis=1)
    nps = np.concatenate([rs[c]["nps"] for c in range(NCORES)], axis=1)
    ncs = np.concatenate([rs[c]["ncs"] for c in range(NCORES)], axis=1)
    out = (y_prompt, y_sample, nap, npp, ncp, nas, nps, ncs)
    return tuple(np.ascontiguousarray(o, dtype=np.float32) for o in out)
```

```python
import numpy as np
from contextlib import ExitStack
import concourse.bass as bass
import concourse.mybir as mybir
from concourse.bass_utils import run_bass_kernel_spmd

F32 = mybir.dt.float32
BF16 = mybir.dt.bfloat16
ALU = mybir.AluOpType
AF = mybir.ActivationFunctionType
AX = mybir.AxisListType

NCORES = 8
D = 1024
SEQ = 2048
NS_TOK = 16
TP = SEQ + NS_TOK
SEGW = 1040
DA, DB, DC = 384, 256, 384
DIN = 2176
DFF = 4096
DPLE = 256
DEPTH = 2
EPS = 1e-6
WIN = (2, 4, 8, 16)
NSLOT = 8

R_GMIX, R_GMLP, R_GPLE, R_GFIN = 0, 2, 4, 6
R_CAB, R_LNG, R_LNB, R_PSC, R_CCW = 7, 9, 11, 13, 15
NPROW = 24


class Buf:
    __slots__ = ("name", "w", "r", "lo", "hi", "over")

    def __init__(self, name, lo=None, hi=None):
        self.name = name
        self.w = None
        self.r = {}
        self.lo = lo
        self.hi = hi
        self.over = []


class Tracker:
    def __init__(self, nc, es, n_dma_sems=10):
        self.nc = nc
        self.dry = False
        self.eng = {"pe": nc.tensor, "act": nc.scalar, "dve": nc.vector, "pool": nc.gpsimd, "sp": nc.sync}
        self.sem = {}
        self.cnt = {}
        for e in ("pe", "act", "dve", "pool"):
            self.sem[e] = es.enter_context(nc.semaphore("s_" + e))
            self.cnt[e] = 0
        self.dsem = {}
        for q in ("pool", "sp"):
            self.dsem[q] = [[es.enter_context(nc.semaphore("d_%s%d" % (q, i))), 0] for i in range(n_dma_sems)]
        self.dnext = {"pool": 0, "sp": 0}
        self.known = {e: {} for e in self.eng}
        self.arena = []
        self.nwait = 0
        self.nins = 0

    def abuf(self, name, lo, hi):
        b = Buf(name, lo, hi)
        for o in self.arena:
            if o.lo < hi and lo < o.hi:
                o.over.append(b)
                b.over.append(o)
        self.arena.append(b)
        return b

    def _wait(self, e, evs):
        best = {}
        kn = self.known[e]
        for (sem, val, src) in evs:
            k = id(sem)
            if kn.get(k, 0) >= val:
                continue
            if k not in best or best[k][1] < val:
                best[k] = (sem, val)
        for k, (sem, val) in best.items():
            self.eng[e].wait_ge(sem, val)
            kn[k] = val
            self.nwait += 1

    def deps(self, e, reads, writes):
        evs = []
        for b in reads:
            for x in [b] + b.over:
                if x.w is not None:
                    evs.append(x.w)
        for b in writes:
            for x in [b] + b.over:
                if x.w is not None and (x.w[2] != e or e != "pe"):
                    evs.append(x.w)
                for r in x.r.values():
                    if r[2] != e or e != "pe":
                        evs.append(r)
        return evs

    def record(self, ev, reads, writes):
        k = id(ev[0])
        for b in reads:
            b.r[k] = ev
        for b in writes:
            b.w = ev
            b.r = {}

    def op(self, e, reads, writes, emit):
        if self.dry:
            return None
        self._wait(e, self.deps(e, reads, writes))
        ins = emit()
        self.cnt[e] += 1
        self.nins += 1
        ins.then_inc(self.sem[e], 1)
        ev = (self.sem[e], self.cnt[e], e)
        self.record(ev, reads, writes)
        return ev

    def dma(self, q, reads, writes, out, in_, **kw):
        if self.dry:
            return None
        i = self.dnext[q]
        self.dnext[q] = (i + 1) % len(self.dsem[q])
        slot = self.dsem[q][i]
        evs = self.deps("dma:" + q, reads, writes)
        if slot[1] > 0:
            evs.append((slot[0], slot[1], "dma"))
        self._wait(q, evs)
        slot[1] += 16
        self.eng[q].dma_start(out=out, in_=in_, **kw).then_inc(slot[0], 16)
        ev = (slot[0], slot[1], "dma")
        self.record(ev, reads, writes)
        return ev


class TileD:
    def __init__(self, idx, h0, w, lo, samp, li):
        self.idx, self.h0, self.w, self.lo, self.samp, self.li = idx, h0, w, lo, samp, li


SEG_TILES = [
    [TileD(0, 0, 512, 0, False, 0), TileD(1, 512, 512, 512, False, 1)],
    [TileD(2, 1024, 512, 0, False, 0), TileD(3, 1536, 512, 512, False, 1), TileD(4, 2048, NS_TOK, 1024, True, 2)],
]


class Builder:
    def __init__(self):
        self.nc = bass.Bass("TRN2", target_bir_lowering=False)
        self.es = ExitStack()

    def dram_in(self, name, shape):
        return self.nc.dram_tensor(name, list(shape), F32, kind="ExternalInput").ap()

    def dram_out(self, name, shape):
        return self.nc.dram_tensor(name, list(shape), F32, kind="ExternalOutput").ap()

    def sb(self, name, shape, dt):
        return self.es.enter_context(self.nc.sbuf_tensor(name, list(shape), dt))

    def declare(self):
        di, do = self.dram_in, self.dram_out
        self.xp = di("xp", [SEQ, D]); self.xs = di("xs", [NS_TOK, D])
        self.sca = di("sca", [DEPTH, NS_TOK, 30, DA]); self.spl = di("spl", [DEPTH, NS_TOK, 15, DB])
        self.scc = di("scc", [DEPTH, NS_TOK, 2, DC])
        self.pp = di("pp", [DEPTH, SEQ, DPLE]); self.psm = di("psm", [DEPTH, NS_TOK, DPLE])
        self.g_mix = di("g_mix", [DEPTH, D]); self.g_mlp = di("g_mlp", [DEPTH, D]); self.g_ple = di("g_ple", [DEPTH, D])
        self.g_fin = di("g_fin", [1, D])
        self.caw = di("caw", [DEPTH, 31, DA]); self.cab = di("cab", [DEPTH, DA])
        self.lng = di("lng", [DEPTH, DA]); self.lnb = di("lnb", [DEPTH, DA])
        self.plw = di("plw", [DEPTH, 4, 64, 64]); self.psc = di("psc", [DEPTH, DB])
        self.ccw = di("ccw", [DEPTH, 3, DC])
        self.wd = {
            "w_in": di("w_in", [DEPTH, D, DIN]), "w_out": di("w_out", [DEPTH, D, D]),
            "w_up": di("w_up", [DEPTH, D, DFF]), "w_down": di("w_down", [DEPTH, DFF, D]),
            "w_pg": di("w_pg", [DEPTH, D, D]), "w_pp": di("w_pp", [DEPTH, DPLE, D]),
        }
        self.yp = do("yp", [SEQ, D]); self.ys = do("ys", [NS_TOK, D])
        self.nap = do("nap", [DEPTH, 30, DA]); self.npp = do("npp", [DEPTH, 15, DB]); self.ncp = do("ncp", [DEPTH, 2, DC])
        self.nas = do("nas", [DEPTH, NS_TOK, 30, DA]); self.nps = do("nps", [DEPTH, NS_TOK, 15, DB])
        self.ncs = do("ncs", [DEPTH, NS_TOK, 2, DC])

    def alloc(self):
        nc, sb = self.nc, self.sb
        self.T = Tracker(nc, self.es)
        T = self.T
        self.H = sb("H", [128, 8, TP], F32)
        self.Hb = [[Buf("H%d_%d" % (c, t)) for t in range(5)] for c in range(8)]
        self.N = sb("N", [128, 8, SEGW], BF16)
        self.Nb = [[Buf("N%d_%d" % (c, t)) for t in range(3)] for c in range(8)]
        self.R = sb("R", [128, 32 * SEGW], BF16)
        self.HID = self.R[:, :].rearrange("p (c t) -> p c t", c=32)
        self.HIDb = [[T.abuf("HID%d_%d" % (m, t.li), (m * SEGW + t.lo) * 2, (m * SEGW + t.lo + t.w) * 2)
                      for t in SEG_TILES[1]] for m in range(32)]
        self._off = 0

        def carve(shape, dt):
            n = 1
            for s in shape:
                n *= s
            esz = 4 if dt == F32 else 2
            lo = self._off
            hi = lo + n * esz
            self._off = (hi + 31) // 32 * 32
            assert self._off <= 32 * SEGW * 2, self._off
            v = self.R[:, lo // 2:hi // 2]
            if dt == F32:
                v = v.bitcast(F32)
            if len(shape) == 2:
                v = v.rearrange("p (a b) -> p a b", a=shape[0])
            return v, lo, hi

        self.MIX, lo, hi = carve([8, SEGW], BF16)
        self.MIXb = [[T.abuf("MIX%d_%d" % (c, t.li), lo + (c * SEGW + t.lo) * 2, lo + (c * SEGW + t.lo + t.w) * 2)
                      for t in SEG_TILES[1]] for c in range(8)]
        self.GBS, lo, hi = carve([3, 30 + 1024], BF16)
        self.GPXb = [T.abuf("GPX%d" % j, lo + j * 1054 * 2, lo + (j * 1054 + 30) * 2) for j in range(3)]
        self.GPb = [[T.abuf("GP%d_%d" % (j, li), lo + (j * 1054 + 30 + li * 512) * 2, lo + (j * 1054 + 30 + li * 512 + 512) * 2)
                     for li in range(2)] for j in range(3)]
        self.VS, lo, hi = carve([2 + 1024], F32)
        self.VSb = T.abuf("VS", lo, hi)
        self.US, lo, hi = carve([15 + 1024], F32)
        self.USb = T.abuf("US", lo, hi)
        self.TT = []
        self.TTb = []
        for i in range(4):
            v, lo, hi = carve([527], F32)
            self.TT.append(v); self.TTb.append(T.abuf("TT%d" % i, lo, hi))
            self._tt_lo = lo
            if i == 0:
                self._tt0_lo = lo
        self.SIG = []
        self.SIGb = []
        for i in range(2):
            v, lo, hi = carve([512], F32)
            self.SIG.append(v); self.SIGb.append(T.abuf("SIG%d" % i, lo, hi))
        self.signext = 0
        self.YA, self.YAb, self.YS, self.YSb, self.PLD, self.PLDb = [], [], [], [], [], []
        self._ya_lo, self._ys_lo = [], []
        for li in range(2):
            v, lo, hi = carve([3, 512], F32)
            self.YA.append(v); self.YAb.append(T.abuf("YA%d" % li, lo, hi))
            self._ya_lo.append(lo)
            if li == 1:
                self.PS = self.R[:, lo // 2:(lo + 4096) // 2].bitcast(F32).rearrange("p (s d) -> p s d", s=4)
                self.PSb = T.abuf("PS", lo, lo + 4096)
            v, lo, hi = carve([3, 512], BF16)
            self.YS.append(v); self.YSb.append(T.abuf("YS%d" % li, lo, hi))
            self._ys_lo.append(lo)
            v, lo, hi = carve([2, 512], BF16)
            self.PLD.append(v); self.PLDb.append(T.abuf("PLD%d" % li, lo, hi))
        def ov(name, lo, np_, shape):
            n = 1
            for q in shape:
                n *= q
            v = self.R[0:np_, lo // 2:(lo + n * 4) // 2].bitcast(F32)
            if len(shape) == 2:
                v = v.rearrange("p (a b) -> p a b", a=shape[0])
            return v, T.abuf(name, lo, lo + n * 4)
        self.PST, self.PSTb = ov("PST", 32768, NPROW, [D])
        self.CAWS, self.CAWSb = ov("CAWS", 36864, 32, [DEPTH, DA])
        self.XSS, self.XSSb = ov("XSS", 40960, 16, [D])
        self.PSS, self.PSSb = ov("PSS", self._ya_lo[1] + 4096, 16, [DPLE])
        self.SST, self.SSTb = ov("SST", self._ys_lo[1], 128, [DA])
        self.OST, self.OSTb = ov("OST", self._ya_lo[0], 46, [DA])
        ost1, ost1b = ov("OST1", self._ya_lo[0] + 1536, 46, [DA])
        ost2, ost2b = ov("OST2", self._ya_lo[0] + 3072, 46, [DA])
        self.OSTL = [(self.OST, self.OSTb), (ost1, ost1b), (ost2, ost2b)]
        self.TMPS, self.TMPSb = ov("TMPS", self._tt_lo, 128, [4, 30])
        ps2, ps2b = ov("PS2", self._tt0_lo, 128, [4, 256])
        self.PSL = [(self.PS, self.PSb), (ps2, ps2b)]
        self.SST2 = []
        for q in range(2):
            v_, b_ = ov("SSTq%d" % q, self._ys_lo[1] + q * 1536, 128, [DA])
            self.SST2.append((v_, b_))
        self.SQX = self.R[:, 45056 // 2:(45056 + 8192) // 2].rearrange("p (c t) -> p c t", c=8)
        self.SQXb = [T.abuf("SQX%d" % c, 45056 + c * 1024, 45056 + (c + 1) * 1024) for c in range(8)]
        self.XST = [self.R[:, i * 8192:(i + 1) * 8192].bitcast(F32).rearrange("p (s d) -> p s d", s=4) for i in range(2)]
        self.XSTb = [T.abuf("XST%d" % i, i * 16384, (i + 1) * 16384) for i in range(2)]
        self.YSTb = [T.abuf("YST%d" % q, q * 4096, (q + 1) * 4096) for q in range(4)]
        self.ystnext = 0
        self.WS = [sb("WS%d" % i, [128, 8, 128], BF16) for i in range(NSLOT)]
        self.WSb = [Buf("WS%d" % i) for i in range(NSLOT)]
        self.DIAG = [sb("DIAG%d" % i, [128, 31, 128], BF16) for i in range(3)]
        self.DIAGb = [Buf("DIAG%d" % i) for i in range(3)]
        self.PT = sb("PT", [128, 2, SEGW], BF16)
        self.PTb = [Buf("PT%d" % i) for i in range(3)]
        self.PARAM = sb("PARAM", [128, 8, NPROW], F32); self.PARAMb = Buf("PARAM")
        self.PARAMN = sb("PARAMN", [128, 3, 4], F32)
        self.CST = sb("CST", [128, 2], F32); self.CSTb = Buf("CST")
        self.CAWT = sb("CAWT", [128, DEPTH, 3, 31], F32); self.CAWTb = Buf("CAWT")
        self.IDF = sb("IDF", [128, 128], F32); self.IDFb = Buf("IDF")
        self.IDB = sb("IDB", [128, 128], BF16); self.IDBb = Buf("IDB")
        self.ONF = sb("ONF", [128, 128], F32); self.ONB = sb("ONB", [128, 128], BF16); self.ONb = Buf("ON")
        self.PW = sb("PW", [128, DEPTH, 2, 128], BF16); self.PWb = Buf("PW")
        self.INVW = sb("INVW", [128, 2], F32); self.INVC = sb("INVC", [128, 2, 16], F32); self.INVb = Buf("INV")
        self.RS = [sb("RS%d" % i, [128, 512], F32) for i in range(3)]
        self.RSb = [Buf("RS%d" % i) for i in range(3)]
        self.rsnext = 0
        self.RL = [sb("RL%d" % i, [128, 512], F32) for i in range(2)]
        self.RLb = [Buf("RL%d" % i) for i in range(2)]
        self.rlnext = 0
        self.SQ = [sb("SQ%d" % i, [128, 512], BF16) for i in range(2)]
        self.SQb = [Buf("SQ%d" % i) for i in range(2)]
        self.sqnext = 0
        self.GC = sb("GC", [128, 3, 30], BF16); self.GCb = Buf("GC")
        self.VC = sb("VC", [128, 3, 2], F32); self.VCb = Buf("VC")
        self.UC = sb("UC", [128, 2, 15], F32); self.UCb = Buf("UC")
        self.STA = sb("STA", [128, 3, 46], F32); self.STAb = Buf("STA")
        self.STU = sb("STU", [128, 2, 31], F32); self.STUb = Buf("STU")
        self.STV = sb("STV", [128, 3, 18], F32); self.STVb = Buf("STV")
        self.YAS = sb("YAS", [128, 3, 16], F32); self.YASb = Buf("YAS")
        self.YSS = sb("YSS", [128, 3, 16], BF16); self.YSSb = Buf("YSS")
        self.SSS = sb("SSS", [128, 2, 16], F32); self.SSSb = Buf("SSS")
        self.PLS = sb("PLS", [128, 2, 16], BF16); self.PLSb = Buf("PLS")
        self.SCT = sb("SCT", [128, 3, 32], F32); self.SCTb = Buf("SCT")
        self.SMT = sb("SMT", [128, 3, 16], F32); self.SMTb = Buf("SMT")
        self.LNS = sb("LNS", [128, 2, 16], F32); self.LNSb = [Buf("LNS0"), Buf("LNS1")]
        self.PB = [self.es.enter_context(nc.psum_tensor("pb%d" % i, [128, 512], F32)) for i in range(8)]
        self.PBb = [Buf("pb%d" % i) for i in range(8)]
        self.pbnext = 0
        self.out_evs = []
        self.deferred = []
        self.deferred_lo = []
        self._xsubs = {}
        self.worder = []
        self.wpos = 0
        self.wissued = 0
        self.wreleased = 0

    def bank(self):
        i = self.pbnext
        self.pbnext = (i + 1) % 8
        return self.PB[i], self.PBb[i]

    def ring(self, which):
        if which == "sig":
            i = self.signext; self.signext = (i + 1) % 2
            return self.SIG[i], self.SIGb[i]
        if which == "rs":
            i = self.rsnext; self.rsnext = (i + 1) % 3
            return self.RS[i], self.RSb[i]
        if which == "rl":
            i = self.rlnext; self.rlnext = (i + 1) % 2
            return self.RL[i], self.RLb[i]
        i = self.sqnext; self.sqnext = (i + 1) % 2
        return self.SQ[i], self.SQb[i]

    def w_acquire(self, spec):
        if self.T.dry:
            self.worder.append(spec)
            return 0
        idx = self.wpos
        self.wpos += 1
        assert self.worder[idx] == spec, (idx, self.worder[idx], spec)
        self.w_pump()
        assert self.wissued > idx
        return idx % NSLOT

    def w_release(self, n=1):
        if self.T.dry:
            return
        self.wreleased += n
        self.w_pump()

    def w_pump(self):
        while self.wissued < len(self.worder) and self.wissued < self.wreleased + NSLOT:
            i = self.wissued
            name, l, r0, kc, c0 = self.worder[i]
            s = i % NSLOT
            src = self.wd[name][l, r0:r0 + kc * 128, c0:c0 + 128].rearrange("(kc p) n -> p kc n", p=128)
            self.T.dma("pool", [], [self.WSb[s]], self.WS[s][:, 0:kc, :], src)
            self.wissued += 1

    def prow(self, c, row):
        return self.PARAM[:, c, row:row + 1]

    def nprow(self, c, row):
        if R_LNG <= row < R_LNG + DEPTH:
            k = row - R_LNG
        else:
            k = 2 + row - R_LNB
        return self.PARAMN[:, c, k:k + 1]

    def act_sigmoid(self, out, outb, in_, inbufs, nscale=None, nbias=None):
        nc, T = self.nc, self.T
        kw = {}
        if nbias is not None:
            kw["bias"] = nbias
        sc = nscale if nscale is not None else -1.0
        T.op("act", list(inbufs) + [self.PARAMb], [outb], lambda: nc.scalar.activation(out, in_, AF.Exp, scale=sc, **kw))
        T.op("act", [outb, self.CSTb], [outb], lambda: nc.scalar.activation(out, out, AF.Ln, bias=self.CST[:, 1:2]))
        T.op("act", [outb], [outb], lambda: nc.scalar.activation(out, out, AF.Exp, scale=-1.0))

    def act_rsqrt(self, out, outb, in_, inbufs, scale):
        nc, T = self.nc, self.T
        T.op("act", list(inbufs) + [self.CSTb], [outb], lambda: nc.scalar.activation(
            out, in_, AF.Ln, scale=scale, bias=self.CST[:, 0:1]))
        T.op("act", [outb], [outb], lambda: nc.scalar.activation(out, out, AF.Exp, scale=-0.5))

    def mm(self, slot, tile, nk=8, rhs=None, rbufs=None, korder=None, perk=False):
        nc = self.nc
        bk, bb = self.bank()
        w, lo = tile.w, tile.lo
        if rhs is None:
            rhs = lambda k: self.N[:, k, lo:lo + w]
            rbufs = [self.Nb[k][tile.li] for k in range(nk)]
        ko = korder if korder is not None else list(range(nk))
        if perk and len(rbufs) == nk:
            for n_, k in enumerate(ko):
                self.T.op("pe", [self.WSb[slot], rbufs[k]], [bb], lambda n_=n_, k=k: nc.tensor.matmul(
                    bk[:, 0:w], self.WS[slot][:, k, :], rhs(k), start=(n_ == 0), stop=(n_ == nk - 1)))
            return bk, bb

        def emit():
            last = None
            for n_, k in enumerate(ko):
                last = nc.tensor.matmul(bk[:, 0:w], self.WS[slot][:, k, :], rhs(k), start=(n_ == 0), stop=(n_ == nk - 1))
            return last
        self.T.op("pe", [self.WSb[slot]] + rbufs, [bb], emit)
        return bk, bb

    def build_identity(self):
        nc, T = self.nc, self.T
        T.op("pool", [], [self.IDFb], lambda: nc.gpsimd.memset(self.IDF[:], 1.0))
        T.op("pool", [self.IDFb], [self.IDFb], lambda: nc.gpsimd.affine_select(
            out=self.IDF[:], in_=self.IDF[:], pattern=[[-1, 128]], compare_op=ALU.is_equal, fill=0.0,
            base=0, channel_multiplier=1))

    def prologue(self):
        nc, T = self.nc, self.T
        T.op("dve", [], [self.PSTb], lambda: nc.vector.memset(self.PST[:], 0.0))
        T.op("dve", [], [self.CAWSb], lambda: nc.vector.memset(self.CAWS[:], 0.0))
        self.load_x(range(1), phase="dma")
        groups = [(R_GMIX, self.g_mix, DEPTH, D), (R_GMLP, self.g_mlp, DEPTH, D), (R_GPLE, self.g_ple, DEPTH, D),
                  (R_GFIN, self.g_fin, 1, D), (R_CAB, self.cab, DEPTH, DA), (R_LNG, self.lng, DEPTH, DA),
                  (R_LNB, self.lnb, DEPTH, DA), (R_PSC, self.psc, DEPTH, DB),
                  (R_CCW, self.ccw.rearrange("l k c -> (l k) c"), DEPTH * 3, DC)]
        self.PSTrb = []
        for (r, src, nr, n) in groups:
            rb_ = Buf("PSTr%d" % r)
            rb_.w = self.PSTb.w
            self.PSTrb.append(rb_)
            T.dma("sp", [], [rb_], self.PST[r:r + nr, 0:n], src)
        self.CAWSrb = []
        for l in range(DEPTH):
            rb_ = Buf("CAWSr%d" % l)
            rb_.w = self.CAWSb.w
            self.CAWSrb.append(rb_)
            T.dma("sp", [], [rb_], self.CAWS[0:31, l, :], self.caw[l])
        T.op("dve", [self.IDFb], [self.IDBb], lambda: nc.vector.tensor_copy(self.IDB[:], self.IDF[:]))
        T.op("dve", [], [self.ONb], lambda: nc.vector.memset(self.ONF[:], 1.0))
        T.op("dve", [], [self.ONb], lambda: nc.vector.memset(self.ONB[:], 1.0))
        for i, (wa, wb) in enumerate(((2, 4), (8, 16))):
            T.op("dve", [], [self.INVb], lambda i=i, wa=wa: nc.vector.memset(self.INVW[0:64, i:i + 1], 1.0 / wa))
            T.op("dve", [], [self.INVb], lambda i=i, wb=wb: nc.vector.memset(self.INVW[64:128, i:i + 1], 1.0 / wb))
        for t in range(16):
            T.op("dve", [], [self.INVb], lambda t=t: nc.vector.memset(self.INVC[:, :, t:t + 1], 1.0 / (t + 1)))
        for i in range(2):
            T.op("dve", [self.INVb], [self.INVb], lambda i=i: nc.vector.tensor_scalar(
                self.INVC[:, i, :], self.INVC[:, i, :], self.INVW[:, i:i + 1], None, op0=ALU.max))
        for c in range(8):
            bk, bb = self.bank()
            T.op("pe", [self.PSTb, self.IDFb] + self.PSTrb, [bb], lambda c=c, bk=bk: nc.tensor.transpose(
                bk[:, 0:NPROW], self.PST[:, c * 128:(c + 1) * 128], self.IDF[0:NPROW, 0:NPROW]))
            T.op("act", [bb], [self.PARAMb], lambda c=c, bk=bk: nc.scalar.copy(self.PARAM[:, c, :], bk[:, 0:NPROW]))
        T.op("dve", [], [self.CSTb], lambda: nc.vector.memset(self.CST[:, 0:1], EPS))
        T.op("dve", [], [self.CSTb], lambda: nc.vector.memset(self.CST[:, 1:2], 1.0))
        T.op("dve", [self.PARAMb], [self.PARAMb], lambda: nc.vector.tensor_scalar(
            self.PARAMN[:, :, 0:2], self.PARAM[:, 0:3, R_LNG:R_LNG + 2], -1.0, None, op0=ALU.mult))
        T.op("dve", [self.PARAMb], [self.PARAMb], lambda: nc.vector.tensor_scalar(
            self.PARAMN[:, :, 2:4], self.PARAM[:, 0:3, R_LNB:R_LNB + 2], -1.0, None, op0=ALU.mult))
        for l in range(DEPTH):
            for j in range(3):
                bk, bb = self.bank()
                T.op("pe", [self.CAWSb, self.IDFb] + self.CAWSrb, [bb], lambda l=l, j=j, bk=bk: nc.tensor.transpose(
                    bk[:, 0:32], self.CAWS[:, l, j * 128:(j + 1) * 128], self.IDF[0:32, 0:32]))
                T.op("act", [bb], [self.CAWTb], lambda l=l, j=j, bk=bk: nc.scalar.copy(self.CAWT[:, l, j, :], bk[:, 0:31]))
        T.op("dve", [], [self.PWb], lambda: nc.vector.memset(self.PW[:], 0.0))
        for l in range(DEPTH):
            for g in range(4):
                i, h = g // 2, g % 2
                T.dma("pool", [], [self.PWb], self.PW[h * 64:(h + 1) * 64, l, i, h * 64:(h + 1) * 64], self.plw[l, g])
        self.load_x(range(1), phase="tr")
        self.rmsnorm([SEG_TILES[0][0]], R_GMIX + 0, presq="xnow")
        self.load_x(range(1, 2))
        self.rmsnorm([SEG_TILES[0][1]], R_GMIX + 0, presq="xnow")

    def load_x(self, trange, sample=False, phase="both"):
        nc, T = self.nc, self.T
        for t in trange:
            st, stb = self.XST[t % 2], self.XSTb[t % 2]
            if phase in ("both", "dma"):
                subs = []
                for q in range(4):
                    sbf = Buf("XSTs%d_%d" % (t, q))
                    sbf.w = stb.w
                    subs.append(sbf)
                for q in range(4):
                    if q == 0:
                        ev0 = T.dma("sp", [], [stb], st[:, q, :], self.xp[t * 512 + q * 128:t * 512 + (q + 1) * 128, :])
                        subs[0].w = ev0
                    else:
                        T.dma("sp", [], [subs[q]], st[:, q, :], self.xp[t * 512 + q * 128:t * 512 + (q + 1) * 128, :])
                self._xsubs[t] = subs
                if phase == "dma":
                    continue
            subs = self._xsubs[t]
            for c in range(8):
                bk, bb = self.bank()

                def emit(c=c, bk=bk, st=st):
                    last = None
                    for s in range(4):
                        last = nc.tensor.transpose(bk[:, s * 128:(s + 1) * 128], st[:, s, c * 128:(c + 1) * 128], self.IDF[:])
                    return last
                T.op("pe", [stb, self.IDFb] + subs, [bb], emit)
                if c % 2 == 0:
                    T.op("act", [bb], [self.Hb[c][t]], lambda c=c, t=t, bk=bk: nc.scalar.copy(
                        self.H[:, c, t * 512:(t + 1) * 512], bk[:]))
                else:
                    T.op("dve", [bb], [self.Hb[c][t]], lambda c=c, t=t, bk=bk: nc.vector.tensor_copy(
                        self.H[:, c, t * 512:(t + 1) * 512], bk[:]))
        if not sample or phase == "dma":
            return
        T.dma("sp", [], [self.XSSb], self.XSS[:], self.xs)
        for c in range(8):
            bk, bb = self.bank()
            T.op("pe", [self.XSSb, self.IDFb], [bb], lambda c=c, bk=bk: nc.tensor.transpose(
                bk[:, 0:16], self.XSS[:, c * 128:(c + 1) * 128], self.IDF[0:16, 0:16]))
            T.op("act", [bb], [self.Hb[c][4]], lambda c=c, bk=bk: nc.scalar.copy(self.H[:, c, SEQ:TP], bk[:, 0:16]))

    def rmsnorm(self, tiles, grow, final=False, presq=False):
        nc, T = self.nc, self.T
        rs = {}
        for t in tiles:
            bk, bb = self.bank()
            w = t.w
            if not presq and w <= 64:
                sq, sqb = self.ring("sq")
                for c in range(8):
                    T.op("act", [self.Hb[c][t.idx]], [sqb], lambda c=c, t=t, sq=sq, w=w: nc.scalar.activation(
                        sq[:, c * w:(c + 1) * w], self.H[:, c, t.h0:t.h0 + w], AF.Square))
                for c in range(8):
                    T.op("pe", [sqb, self.ONb], [bb], lambda c=c, bk=bk, sq=sq, w=w: nc.tensor.matmul(
                        bk[:, 0:w], self.ONB[:], sq[:, c * w:(c + 1) * w], start=(c == 0), stop=(c == 7)))
                r, rb = self.ring("rs")
                self.act_rsqrt(r[:, 0:w], rb, bk[:, 0:w], [bb], 1.0 / D)
                rs[t.idx] = (r, rb)
                continue
            for c in range(8):
                if presq == "xnow":
                    sq, sqb = self.SQX[:, c, 0:w], self.SQXb[c]
                    T.op("act", [self.Hb[c][t.idx]], [sqb], lambda c=c, t=t, sq=sq, w=w: nc.scalar.activation(
                        sq[:, 0:w], self.H[:, c, t.h0:t.h0 + w], AF.Square))
                elif presq == "x":
                    sq, sqb = self.SQX[:, c, 0:w], self.SQXb[c]
                elif presq:
                    sq, sqb = self.N[:, c, t.lo:t.lo + w], self.Nb[c][t.li]
                else:
                    sq, sqb = self.ring("sq")
                    T.op("act", [self.Hb[c][t.idx]], [sqb], lambda c=c, t=t, sq=sq, w=w: nc.scalar.activation(
                        sq[:, 0:w], self.H[:, c, t.h0:t.h0 + w], AF.Square))
                T.op("pe", [sqb, self.ONb], [bb], lambda c=c, bk=bk, sq=sq, w=w: nc.tensor.matmul(
                    bk[:, 0:w], self.ONB[:], sq[:, 0:w], start=(c == 0), stop=(c == 7)))
            r, rb = self.ring("rs")
            self.act_rsqrt(r[:, 0:w], rb, bk[:, 0:w], [bb], 1.0 / D)
            rs[t.idx] = (r, rb)
        for t in tiles:
            r, rb = rs[t.idx]
            w = t.w
            for c in range(8):
                if final:
                    T.op("dve", [self.Hb[c][t.idx], rb, self.PARAMb], [self.Hb[c][t.idx]],
                         lambda c=c, t=t, r=r, w=w: nc.vector.scalar_tensor_tensor(
                             self.H[:, c, t.h0:t.h0 + w], self.H[:, c, t.h0:t.h0 + w], self.prow(c, grow), r[:, 0:w],
                             op0=ALU.mult, op1=ALU.mult))
                else:
                    T.op("dve", [self.Hb[c][t.idx], rb, self.PARAMb], [self.Nb[c][t.li]],
                         lambda c=c, t=t, r=r, w=w: nc.vector.scalar_tensor_tensor(
                             self.N[:, c, t.lo:t.lo + w], self.H[:, c, t.h0:t.h0 + w], self.prow(c, grow), r[:, 0:w],
                             op0=ALU.mult, op1=ALU.mult))

    def p_dma(self, l, seg, tiles):
        T = self.T
        for t in tiles:
            if t.samp:
                T.dma("sp", [], [self.PSSb], self.PSS[:], self.psm[l])
            else:
                ps, psb = self.PSL[t.li]
                T.dma("sp", [], [psb], ps[:], self.pp[l, t.h0:t.h0 + 512, :].rearrange("(s p) d -> p s d", p=128))

    def p_transposes(self, l, seg, tiles):
        nc, T = self.nc, self.T
        for t in tiles:
            if t.samp:
                for cc in range(2):
                    bk, bb = self.bank()
                    T.op("pe", [self.PSSb, self.IDFb], [bb], lambda cc=cc, bk=bk: nc.tensor.transpose(
                        bk[:, 0:16], self.PSS[:, cc * 128:(cc + 1) * 128], self.IDF[0:16, 0:16]))
                    T.op("act", [bb], [self.PTb[2]], lambda cc=cc, bk=bk, t=t: nc.scalar.copy(
                        self.PT[:, cc, t.lo:t.lo + 16], bk[:, 0:16]))
            else:
                ps, psb = self.PSL[t.li]
                for cc in range(2):
                    bk, bb = self.bank()

                    def emit(cc=cc, bk=bk, ps=ps):
                        last = None
                        for s in range(4):
                            last = nc.tensor.transpose(bk[:, s * 128:(s + 1) * 128], ps[:, s, cc * 128:(cc + 1) * 128], self.IDF[:])
                        return last
                    T.op("pe", [psb, self.IDFb], [bb], emit)
                    T.op("act", [bb], [self.PTb[t.li]], lambda cc=cc, bk=bk, t=t: nc.scalar.copy(
                        self.PT[:, cc, t.lo:t.lo + 512], bk[:]))

    def build_diag(self, l, j, slot):
        nc, T = self.nc, self.T
        T.op("dve", [self.IDBb, self.CAWTb], [self.DIAGb[slot]], lambda: nc.vector.tensor_tensor(
            self.DIAG[slot][:], self.IDB[:].unsqueeze(1).broadcast_to([128, 31, 128]),
            self.CAWT[:, l, j, :].unsqueeze(2).broadcast_to([128, 31, 128]), op=ALU.mult))

    def sample_state_steps(self, l):
        nc, T = self.nc, self.T
        steps = []

        def stg(k):
            return self.SST2[k % 2]
        k = 0
        for g in range(4):
            def dma(k=k, g=g):
                v, b = stg(k)
                T.dma("sp", [], [b], v[0:120, :], self.sca[l, 4 * g:4 * g + 4].rearrange("b k c -> (b k) c"))

            def comp(k=k, g=g):
                v, b = stg(k)
                for j in range(3):
                    bk, bb = self.bank()
                    T.op("pe", [b, self.IDFb], [bb], lambda j=j, bk=bk: nc.tensor.transpose(
                        bk[:, 0:120], v[0:120, j * 128:(j + 1) * 128], self.IDF[0:120, 0:120]))
                    T.op("dve", [bb, self.CAWTb], [self.TMPSb], lambda j=j, bk=bk: nc.vector.tensor_tensor(
                        self.TMPS[:], bk[:, 0:120].rearrange("p (b k) -> p b k", b=4),
                        self.CAWT[:, l, j, 0:30].unsqueeze(1).broadcast_to([128, 4, 30]), op=ALU.mult))
                    T.op("dve", [self.TMPSb], [self.YASb], lambda j=j: nc.vector.tensor_reduce(
                        self.YAS[:, j, 4 * g:4 * g + 4], self.TMPS[:], axis=AX.X, op=ALU.add))
            steps.append((dma, comp)); k += 1
        for g in range(2):
            def dma(k=k, g=g):
                v, b = stg(k)
                T.dma("sp", [], [b], v[0:120, 0:DB], self.spl[l, 8 * g:8 * g + 8].rearrange("b k c -> (b k) c"))

            def comp(k=k, g=g):
                v, b = stg(k)
                for i in range(2):
                    bk, bb = self.bank()
                    T.op("pe", [b, self.IDFb], [bb], lambda i=i, bk=bk: nc.tensor.transpose(
                        bk[:, 0:120], v[0:120, i * 128:(i + 1) * 128], self.IDF[0:120, 0:120]))
                    for h in range(2):
                        wn = WIN[2 * i + h]
                        T.op("dve", [bb], [self.SSSb], lambda i=i, h=h, wn=wn, bk=bk: nc.vector.tensor_reduce(
                            self.SSS[h * 64:(h + 1) * 64, i, 8 * g:8 * g + 8],
                            bk[h * 64:(h + 1) * 64, 0:120].rearrange("p (b k) -> p b k", b=8)[:, :, 15 - (wn - 1):15],
                            axis=AX.X, op=ALU.add))
            steps.append((dma, comp)); k += 1

        def dma(k=k):
            v, b = stg(k)
            T.dma("sp", [], [b], v[0:32, :], self.scc[l].rearrange("b k c -> (b k) c"))

        def comp(k=k):
            v, b = stg(k)
            for j in range(3):
                bk, bb = self.bank()
                T.op("pe", [b, self.IDFb], [bb], lambda j=j, bk=bk: nc.tensor.transpose(
                    bk[:, 0:32], v[0:32, j * 128:(j + 1) * 128], self.IDF[0:32, 0:32]))
                T.op("act", [bb], [self.SCTb], lambda j=j, bk=bk: nc.scalar.copy(self.SCT[:, j, :], bk[:, 0:32]))
        steps.append((dma, comp))
        return steps

    def state_shift_copies(self):
        T = self.T
        for l in range(DEPTH):
            self.out_evs.append(T.dma("sp", [], [], self.nas[l][:, 0:29, :], self.sca[l][:, 1:30, :]))
            self.out_evs.append(T.dma("sp", [], [], self.nps[l][:, 0:14, :], self.spl[l][:, 1:15, :]))
            self.out_evs.append(T.dma("sp", [], [], self.ncs[l][:, 0:1, :], self.scc[l][:, 1:2, :]))

    def flush_deferred(self):
        d, self.deferred = self.deferred, []
        lo, self.deferred_lo = self.deferred_lo, []
        for f, t in d[:1]:
            f(t)
        for f, t in lo:
            f(t)
        for f, t in d[1:]:
            f(t)

    def run_phase(self, specs, tiles, do, post_tile=None, s_tail=0, r_head=0, post_delay=0, defer=True, flush_first=False):
        M = len(specs)
        rest = tiles[1:]

        def acq(m):
            return [self.w_acquire(sp) for sp in specs[m]]
        if flush_first:
            self.flush_deferred()
        if r_head > 0:
            sl = {i: acq(i) for i in range(r_head)}
            for i in range(r_head):
                do(i, sl[i], tiles[0])
                if i == 0:
                    self.flush_deferred()
            for i in range(r_head):
                for t in rest:
                    do(i, sl[i], t)
            self.w_release(sum(len(specs[i]) for i in range(r_head)))
        self.flush_deferred()
        for m in range(r_head, M - s_tail):
            sl1 = acq(m)
            for t in tiles:
                do(m, sl1, t)
            self.w_release(len(specs[m]))
        tail = list(range(M - s_tail, M))
        sl = {i: acq(i) for i in tail}
        for i in tail:
            do(i, sl[i], tiles[0])
        pending = [tiles[0]] if post_tile is not None else []
        for ti, t in enumerate(rest):
            for n_, i in enumerate(tail):
                if pending and (ti > 0 or n_ >= post_delay):
                    post_tile(pending.pop())
                do(i, sl[i], t)
            if pending:
                post_tile(pending.pop())
            if post_tile is not None:
                if defer:
                    self.deferred.append((post_tile, t))
                else:
                    post_tile(t)
        if pending:
            post_tile(pending.pop())
        if tail:
            self.w_release(sum(len(specs[i]) for i in tail))

    def p1(self, l, seg, post_tile):
        nc, T = self.nc, self.T
        tiles = SEG_TILES[seg]
        ptiles = [t for t in tiles if not t.samp]
        stile = tiles[2] if seg == 1 else None
        self.p_dma(l, seg, tiles)
        for j in range(3):
            if seg == 0:
                T.op("dve", [], [self.GPXb[j]], lambda j=j: nc.vector.memset(self.GBS[:, j, 0:30], 0.0))
            else:
                T.op("dve", [self.GCb], [self.GPXb[j]], lambda j=j: nc.vector.tensor_copy(self.GBS[:, j, 0:30], self.GC[:, j, :]))

        def a_proj(j, sl, t):
            sg, sv = sl
            w = t.w
            bg, bgb = self.mm(sg, t, perk=(j == 0))
            bv, bvb = self.mm(sv, t)
            sig, sigb = self.ring("sig")
            self.act_sigmoid(sig[:, 0:w], sigb, bg[:, 0:w], [bgb])
            if not t.samp:
                T.op("dve", [bvb, sigb], [self.GPb[j][t.li]], lambda: nc.vector.tensor_tensor(
                    self.GBS[:, j, 30 + t.lo:30 + t.lo + 512], bv[:], sig[:], op=ALU.mult))
                if seg == 1 and t.li == 1:
                    T.op("dve", [bvb, sigb], [self.STAb], lambda: nc.vector.tensor_tensor(
                        self.STA[:, j, 0:30], bv[:, 482:512], sig[:, 482:512], op=ALU.mult))
            else:
                T.op("dve", [bvb, sigb], [self.STAb], lambda: nc.vector.tensor_tensor(
                    self.STA[:, j, 30:46], bv[:, 0:16], sig[:, 0:16], op=ALU.mult))
        self.run_phase([[("w_in", l, 0, 8, (3 + j) * 128), ("w_in", l, 0, 8, j * 128)] for j in range(3)],
                       tiles, a_proj, r_head=2)
        if seg == 0:
            for j in range(3):
                T.op("dve", [self.GPb[j][1]], [self.GCb], lambda j=j: nc.vector.tensor_copy(
                    self.GC[:, j, :], self.GBS[:, j, 1024:1054]))
        self.p_transposes(l, seg, tiles)
        if l == 0 and seg == 0:
            for j in range(3):
                self.build_diag(0, j, j)
        for i in range(2):
            if seg == 0:
                T.op("dve", [], [self.USb], lambda: nc.vector.memset(self.US[:, 0:15], 0.0))
            else:
                T.op("dve", [self.UCb], [self.USb], lambda i=i: nc.vector.tensor_copy(self.US[:, 0:15], self.UC[:, i, :]))
            su = self.w_acquire(("w_in", l, 0, 8, (6 + i) * 128))
            for t in tiles:
                bu, bub = self.mm(su, t)
                if not t.samp:
                    T.op("act", [bub], [self.USb], lambda bu=bu, t=t: nc.scalar.copy(self.US[:, 15 + t.lo:15 + t.lo + 512], bu[:]))
                else:
                    T.op("act", [bub], [self.STUb], lambda bu=bu, i=i: nc.scalar.copy(self.STU[:, i, 15:31], bu[:, 0:16]))
            self.w_release(1)
            if seg == 0:
                T.op("dve", [self.USb], [self.UCb], lambda i=i: nc.vector.tensor_copy(self.UC[:, i, :], self.US[:, 1024:1039]))
            else:
                T.op("dve", [self.USb], [self.STUb], lambda i=i: nc.vector.tensor_copy(self.STU[:, i, 0:15], self.US[:, 1024:1039]))
            for t in ptiles:
                U = self.US[:, t.lo:t.lo + 527]
                T2, T4, T8, T16 = self.TT
                T.op("dve", [self.USb], [self.TTb[0]], lambda U=U: nc.vector.tensor_tensor(
                    T2[:, 1:527], U[:, 1:527], U[:, 0:526], op=ALU.add))
                T.op("dve", [self.TTb[0]], [self.TTb[1]], lambda: nc.vector.tensor_tensor(
                    T4[:, 3:527], T2[:, 3:527], T2[:, 1:525], op=ALU.add))
                if i == 0:
                    srcs = [(T2, self.TTb[0]), (T4, self.TTb[1])]
                else:
                    T.op("dve", [self.TTb[1]], [self.TTb[2]], lambda: nc.vector.tensor_tensor(
                        T8[:, 7:527], T4[:, 7:527], T4[:, 3:523], op=ALU.add))
                    T.op("dve", [self.TTb[2]], [self.TTb[3]], lambda: nc.vector.tensor_tensor(
                        T16[64:128, 15:527], T8[64:128, 15:527], T8[64:128, 7:519], op=ALU.add))
                    srcs = [(T8, self.TTb[2]), (T16, self.TTb[3])]
                for h in range(2):
                    S, Sb = srcs[h]
                    pr = slice(h * 64, (h + 1) * 64)
                    T.op("dve", [Sb, self.USb, self.INVb], [self.PLDb[t.li]], lambda S=S, U=U, pr=pr, i=i, t=t: nc.vector.scalar_tensor_tensor(
                        self.PLD[t.li][pr, i, :], S[pr, 15:527], self.INVW[pr, i:i + 1], U[pr, 15:527],
                        op0=ALU.mult, op1=ALU.subtract))
                    if t.idx == 0:
                        T.op("dve", [Sb, self.INVb], [self.SMTb], lambda S=S, pr=pr, i=i: nc.vector.tensor_tensor(
                            self.SMT[pr, 0, 0:15], S[pr, 15:30], self.INVC[pr, i, 0:15], op=ALU.mult))
                        T.op("dve", [self.SMTb, self.USb], [self.PLDb[t.li]], lambda U=U, pr=pr, i=i, t=t: nc.vector.tensor_tensor(
                            self.PLD[t.li][pr, i, 0:15], self.SMT[pr, 0, 0:15], U[pr, 15:30], op=ALU.subtract))
            if stile is not None:
                T.op("dve", [self.SSSb, self.STUb], [self.SMTb], lambda i=i: nc.vector.tensor_tensor(
                    self.SMT[:, 1, :], self.SSS[:, i, :], self.STU[:, i, 15:31], op=ALU.add))
                T.op("dve", [self.SMTb, self.STUb, self.INVb], [self.PLSb], lambda i=i: nc.vector.scalar_tensor_tensor(
                    self.PLS[:, i, :], self.SMT[:, 1, :], self.INVW[:, i:i + 1], self.STU[:, i, 15:31],
                    op0=ALU.mult, op1=ALU.subtract))
        def conv(j, t, ds):
            bk, bb = self.bank()
            rb = [self.GPb[j][0], self.GPXb[j]] if t.li == 0 else [self.GPb[j][0], self.GPb[j][1]]

            def emit():
                last = None
                for k in range(31):
                    last = nc.tensor.matmul(bk[:], self.DIAG[ds][:, k, :], self.GBS[:, j, t.lo + k:t.lo + k + 512],
                                            start=(k == 0), stop=(k == 30))
                return last
            T.op("pe", [self.DIAGb[ds]] + rb, [bb], emit)
            T.op("act", [bb, self.PARAMb], [self.YAb[t.li]], lambda: nc.scalar.activation(
                self.YA[t.li][:, j, :], bk[:], AF.Identity, bias=self.prow(j, R_CAB + l)))
            T.op("act", [bb, self.PARAMb], [self.YSb[t.li]], lambda: nc.scalar.activation(
                self.YS[t.li][:, j, :], bk[:], AF.Square, bias=self.prow(j, R_CAB + l)))

        def layer_norm(t):
            w = t.w
            if t.samp:
                ya, yab, ysq, ysb = self.YAS, self.YASb, self.YSS, self.YSSb
            else:
                ya, yab, ysq, ysb = self.YA[t.li], self.YAb[t.li], self.YS[t.li], self.YSb[t.li]
            bs, bsb = self.bank()

            def emit_s():
                last = None
                for j in range(3):
                    last = nc.tensor.matmul(bs[:, 0:w], self.ONF[:], ya[:, j, 0:w], start=(j == 0), stop=(j == 2))
                return last
            T.op("pe", [yab, self.ONb], [bsb], emit_s)
            bq, bqb = self.bank()

            def emit_q():
                last = None
                for j in range(3):
                    last = nc.tensor.matmul(bq[:, 0:w], self.ONB[:], ysq[:, j, 0:w], start=(j == 0), stop=(j == 2))
                return last
            T.op("pe", [ysb, self.ONb], [bqb], emit_q)
            if t.samp:
                mean, meanb = self.LNS[:, 0, :], self.LNSb[0]
                rstd, rstdb = self.LNS[:, 1, :], self.LNSb[1]
            else:
                mean, meanb = self.ring("rs")
                rstd, rstdb = self.ring("rs")
            T.op("act", [bsb], [meanb], lambda: nc.scalar.activation(mean[:, 0:w], bs[:, 0:w], AF.Copy, scale=1.0 / DA))
            T.op("dve", [meanb], [rstdb], lambda: nc.vector.scalar_tensor_tensor(
                rstd[:, 0:w], mean[:, 0:w], -1.0, mean[:, 0:w], op0=ALU.mult, op1=ALU.mult))
            T.op("dve", [bqb, rstdb], [rstdb], lambda: nc.vector.scalar_tensor_tensor(
                rstd[:, 0:w], bq[:, 0:w], 1.0 / DA, rstd[:, 0:w], op0=ALU.mult, op1=ALU.add))
            T.op("dve", [rstdb], [rstdb], lambda: nc.vector.tensor_scalar(
                rstd[:, 0:w], rstd[:, 0:w], 0.0, None, op0=ALU.max))
            self.act_rsqrt(rstd[:, 0:w], rstdb, rstd[:, 0:w], [rstdb], 1.0)
            yield
            T.op("dve", [yab, meanb], [yab], lambda: nc.vector.tensor_tensor(
                ya[:, :, 0:w], ya[:, :, 0:w], mean[:, 0:w].unsqueeze(1).broadcast_to([128, 3, w]), op=ALU.subtract))
            T.op("dve", [yab, rstdb], [yab], lambda: nc.vector.tensor_tensor(
                ya[:, :, 0:w], ya[:, :, 0:w], rstd[:, 0:w].unsqueeze(1).broadcast_to([128, 3, w]), op=ALU.mult))
            yield
            for j in range(3):
                sig, sigb = self.ring("sig")
                self.act_sigmoid(sig[:, 0:w], sigb, ya[:, j, 0:w], [yab],
                                 nscale=self.nprow(j, R_LNG + l), nbias=self.nprow(j, R_LNB + l))
                T.op("dve", [yab, self.PARAMb], [yab], lambda j=j: nc.vector.tensor_scalar(
                    ya[:, j, 0:w], ya[:, j, 0:w], self.prow(j, R_LNG + l), self.prow(j, R_LNB + l), op0=ALU.mult, op1=ALU.add))
                T.op("dve", [yab, sigb], [self.MIXb[j][t.li]], lambda sig=sig, j=j: nc.vector.tensor_tensor(
                    self.MIX[:, j, t.lo:t.lo + w], ya[:, j, 0:w], sig[:, 0:w], op=ALU.mult))
                yield

        live = []

        def pump(n=1):
            for _ in range(n):
                for g in list(live):
                    try:
                        next(g)
                    except StopIteration:
                        live.remove(g)

        tA, tB = ptiles
        for j in range(3):
            conv(j, tA, j)
        conv(0, tB, 0)
        live.append(layer_norm(tA))
        pump(1)
        conv(1, tB, 1)
        pump(2)
        conv(2, tB, 2)
        pump(3)
        live.append(layer_norm(tB))
        if stile is not None:
            for j in range(3):
                T.op("dve", [self.STAb, self.CAWTb, self.YASb], [self.YASb], lambda j=j: nc.vector.scalar_tensor_tensor(
                    self.YAS[:, j, :], self.STA[:, j, 30:46], self.CAWT[:, l, j, 30:31], self.YAS[:, j, :],
                    op0=ALU.mult, op1=ALU.add))
                T.op("dve", [self.YASb, self.PARAMb], [self.YASb], lambda j=j: nc.vector.tensor_scalar(
                    self.YAS[:, j, :], self.YAS[:, j, :], self.prow(j, R_CAB + l), None, op0=ALU.add))
                T.op("dve", [self.YASb], [self.YSSb], lambda j=j: nc.vector.tensor_tensor(
                    self.YSS[:, j, :], self.YAS[:, j, :], self.YAS[:, j, :], op=ALU.mult))
            live.append(layer_norm(stile))
        for i in range(2):
            for t in ptiles:
                bk, bb = self.bank()
                T.op("pe", [self.PLDb[t.li], self.PWb], [bb], lambda bk=bk, i=i, t=t: nc.tensor.matmul(
                    bk[:], self.PW[:, l, i, :], self.PLD[t.li][:, i, :], start=True, stop=True))
                T.op("act", [bb, self.PARAMb], [self.MIXb[3 + i][t.li]], lambda bk=bk, i=i, t=t: nc.scalar.activation(
                    self.MIX[:, 3 + i, t.lo:t.lo + 512], bk[:], AF.Copy, scale=self.prow(i, R_PSC + l)))
            if stile is not None:
                t = stile
                bk, bb = self.bank()
                T.op("pe", [self.PLSb, self.PWb], [bb], lambda bk=bk, i=i: nc.tensor.matmul(
                    bk[:, 0:16], self.PW[:, l, i, :], self.PLS[:, i, :], start=True, stop=True))
                T.op("act", [bb, self.PARAMb], [self.MIXb[3 + i][t.li]], lambda bk=bk, i=i, t=t: nc.scalar.activation(
                    self.MIX[:, 3 + i, t.lo:t.lo + 16], bk[:, 0:16], AF.Copy, scale=self.prow(i, R_PSC + l)))

        def c_proj(j):
            if seg == 0:
                T.op("dve", [], [self.VSb], lambda: nc.vector.memset(self.VS[:, 0:2], 0.0))
            else:
                T.op("dve", [self.VCb], [self.VSb], lambda j=j: nc.vector.tensor_copy(self.VS[:, 0:2], self.VC[:, j, :]))
            sc_ = self.w_acquire(("w_in", l, 0, 8, (11 + j) * 128))
            sx = self.w_acquire(("w_in", l, 0, 8, (14 + j) * 128))
            sbb = self.w_acquire(("w_in", l, 0, 8, (8 + j) * 128))
            w0, w1, w2 = (self.prow(j, R_CCW + l * 3 + k) for k in range(3))
            for t in tiles:
                w = t.w
                bc, bcb = self.mm(sc_, t)
                bx, bxb = self.mm(sx, t)
                bbk, bbb = self.mm(sbb, t)
                cc, ccb = self.ring("sig")
                T.op("act", [bcb], [ccb], lambda bc=bc, cc=cc, w=w: nc.scalar.copy(cc[:, 0:w], bc[:, 0:w]))
                if not t.samp:
                    V = self.VS[:, t.lo:t.lo + 514]
                    T.op("dve", [bxb, ccb], [self.VSb], lambda bx=bx, cc=cc, V=V: nc.vector.tensor_tensor(
                        V[:, 2:514], bx[:], cc[:], op=ALU.mult))
                    tm, tmb = self.ring("sig")
                    T.op("act", [self.VSb, self.PARAMb], [tmb], lambda V=V, tm=tm: nc.scalar.activation(
                        tm[:], V[:, 2:514], AF.Copy, scale=w2))
                    T.op("dve", [self.VSb, self.PARAMb, tmb], [tmb], lambda V=V, tm=tm: nc.vector.scalar_tensor_tensor(
                        tm[:], V[:, 1:513], w1, tm[:], op0=ALU.mult, op1=ALU.add))
                    T.op("dve", [self.VSb, self.PARAMb, tmb], [tmb], lambda V=V, tm=tm: nc.vector.scalar_tensor_tensor(
                        tm[:], V[:, 0:512], w0, tm[:], op0=ALU.mult, op1=ALU.add))
                    T.op("dve", [bbb, tmb], [self.MIXb[5 + j][t.li]], lambda bbk=bbk, tm=tm, j=j, t=t: nc.vector.tensor_tensor(
                        self.MIX[:, 5 + j, t.lo:t.lo + 512], bbk[:], tm[:], op=ALU.mult))
                else:
                    T.op("dve", [bxb, ccb], [self.STVb], lambda bx=bx, cc=cc, j=j: nc.vector.tensor_tensor(
                        self.STV[:, j, 2:18], bx[:, 0:16], cc[:, 0:16], op=ALU.mult))
                    sct = self.SCT[:, j, :].rearrange("p (b k) -> p b k", k=2)
                    T.op("dve", [self.STVb, self.PARAMb], [self.SMTb], lambda j=j: nc.vector.tensor_scalar(
                        self.SMT[:, 2, :], self.STV[:, j, 2:18], w2, None, op0=ALU.mult))
                    T.op("dve", [self.SCTb, self.PARAMb, self.SMTb], [self.SMTb], lambda sct=sct: nc.vector.scalar_tensor_tensor(
                        self.SMT[:, 2, :], sct[:, :, 1], w1, self.SMT[:, 2, :], op0=ALU.mult, op1=ALU.add))
                    T.op("dve", [self.SCTb, self.PARAMb, self.SMTb], [self.SMTb], lambda sct=sct: nc.vector.scalar_tensor_tensor(
                        self.SMT[:, 2, :], sct[:, :, 0], w0, self.SMT[:, 2, :], op0=ALU.mult, op1=ALU.add))
                    T.op("dve", [bbb, self.SMTb], [self.MIXb[5 + j][t.li]], lambda bbk=bbk, j=j, t=t: nc.vector.tensor_tensor(
                        self.MIX[:, 5 + j, t.lo:t.lo + 16], bbk[:, 0:16], self.SMT[:, 2, :], op=ALU.mult))
                pump(2 if j == 0 else 1)
            self.w_release(3)
            if seg == 0:
                T.op("dve", [self.VSb], [self.VCb], lambda j=j: nc.vector.tensor_copy(self.VC[:, j, :], self.VS[:, 1024:1026]))
            else:
                T.op("dve", [self.VSb], [self.STVb], lambda j=j: nc.vector.tensor_copy(self.STV[:, j, 0:2], self.VS[:, 1024:1026]))

        c_proj(0)
        c_proj(1)
        c_proj(2)
        pump(8)
        if seg == 1:
            self.state_outputs(l)
        korder = [3, 4, 5, 6, 0, 1, 2, 7]

        def o_proj(m, sl, t):
            bk, bb = self.mm(sl[0], t, rhs=lambda k: self.MIX[:, k, t.lo:t.lo + t.w],
                             rbufs=[self.MIXb[c][t.li] for c in range(8)], korder=korder, perk=(m == 0))
            T.op("dve", [bb, self.Hb[m][t.idx]], [self.Hb[m][t.idx]], lambda: nc.vector.tensor_tensor(
                self.H[:, m, t.h0:t.h0 + t.w], self.H[:, m, t.h0:t.h0 + t.w], bk[:, 0:t.w], op=ALU.add))
            T.op("act", [self.Hb[m][t.idx]], [self.Nb[m][t.li]], lambda: nc.scalar.activation(
                self.N[:, m, t.lo:t.lo + t.w], self.H[:, m, t.h0:t.h0 + t.w], AF.Square))
        self.run_phase([[("w_out", l, 0, 8, m * 128)] for m in range(8)], tiles, o_proj, post_tile=post_tile, s_tail=4, post_delay=2)

    def state_outputs(self, l):
        nc, T = self.nc, self.T
        specs = [(self.STA, self.STAb, 3, 46, 30, self.nap, self.nas, 29),
                 (self.STU, self.STUb, 2, 31, 15, self.npp, self.nps, 14),
                 (self.STV, self.STVb, 3, 18, 2, self.ncp, self.ncs, 1)]
        for si, (st, stb, nch, n, npr, outp, outs, srow) in enumerate(specs):
            ost, ostb = self.OSTL[si]
            for j in range(nch):
                bk, bb = self.bank()
                T.op("pe", [stb, self.IDFb], [bb], lambda st=st, j=j, n=n, bk=bk: nc.tensor.transpose(
                    bk[0:n, 0:128], st[:, j, 0:n], self.IDF[:]))
                T.op("act", [bb], [ostb], lambda j=j, n=n, bk=bk, ost=ost: nc.scalar.copy(ost[0:n, j * 128:(j + 1) * 128], bk[0:n, 0:128]))
            wd = nch * 128
            self.out_evs.append(T.dma("sp", [ostb], [], outp[l], ost[0:npr, 0:wd]))
            self.out_evs.append(T.dma("sp", [ostb], [], outs[l][:, srow, :], ost[npr:n, 0:wd]))

    def p2(self, l, seg, post_tile, bg=None):
        nc, T = self.nc, self.T
        tiles = SEG_TILES[seg]

        def up(m, sl, t):
            w = t.w
            bk, bb = self.mm(sl[0], t, perk=(m == 0))
            tm, tmb = self.ring("rl")
            hb = self.HIDb[m][t.li]
            if m % 2 == 0:
                T.op("act", [bb], [tmb], lambda: nc.scalar.activation(tm[:, 0:w], bk[:, 0:w], AF.Relu))
                T.op("dve", [tmb], [hb], lambda: nc.vector.tensor_tensor(
                    self.HID[:, m, t.lo:t.lo + w], tm[:, 0:w], tm[:, 0:w], op=ALU.mult))
            else:
                T.op("dve", [bb], [tmb], lambda: nc.vector.tensor_scalar(
                    tm[:, 0:w], bk[:, 0:w], 0.0, None, op0=ALU.max))
                T.op("act", [tmb], [hb], lambda: nc.scalar.activation(
                    self.HID[:, m, t.lo:t.lo + w], tm[:, 0:w], AF.Square))
            if bg is not None and t.li == 0 and m in bg:
                bg[m]()
        self.run_phase([[("w_up", l, 0, 8, m * 128)] for m in range(32)], tiles, up, r_head=5)

        def down(m, sl, t):
            bk, bb = self.bank()
            w = t.w

            def emit():
                last = None
                for k in range(32):
                    last = nc.tensor.matmul(bk[:, 0:w], self.WS[sl[k // 8]][:, k % 8, :], self.HID[:, k, t.lo:t.lo + w],
                                            start=(k == 0), stop=(k == 31))
                return last
            T.op("pe", [self.WSb[q] for q in sl] + [self.HIDb[k][t.li] for k in range(32)], [bb], emit)
            T.op("dve", [bb, self.Hb[m][t.idx]], [self.Hb[m][t.idx]], lambda: nc.vector.tensor_tensor(
                self.H[:, m, t.h0:t.h0 + w], self.H[:, m, t.h0:t.h0 + w], bk[:, 0:w], op=ALU.add))
            T.op("act", [self.Hb[m][t.idx]], [self.Nb[m][t.li]], lambda: nc.scalar.activation(
                self.N[:, m, t.lo:t.lo + w], self.H[:, m, t.h0:t.h0 + w], AF.Square))
        self.run_phase([[("w_down", l, kq * 1024, 8, m * 128) for kq in range(4)] for m in range(8)], tiles, down,
                       post_tile=post_tile, s_tail=1)

    def pre_squares(self, pt):
        nc, T = self.nc, self.T
        for c in range(8):
            T.op("act", [self.Hb[c][pt.idx]], [self.SQXb[c]], lambda c=c: nc.scalar.activation(
                self.SQX[:, c, 0:pt.w], self.H[:, c, pt.h0:pt.h0 + pt.w], AF.Square))

    def p3(self, l, seg, post_tile, hooks=None, pre_sq_tile=None, last_sq=False):
        nc, T = self.nc, self.T
        tiles = SEG_TILES[seg]
        if hooks is not None and -1 in hooks:
            for f in hooks[-1]:
                f()
        if pre_sq_tile is not None:
            self.pre_squares(pre_sq_tile)

        def ple(m, sl, t):
            w = t.w
            bg_, bgb = self.mm(sl[0], t, perk=(m == 0))
            bp, bpb = self.mm(sl[1], t, nk=2, rhs=lambda k: self.PT[:, k, t.lo:t.lo + t.w], rbufs=[self.PTb[t.li]])
            g, gb = self.ring("rl")
            T.op("act", [bgb], [gb], lambda: nc.scalar.activation(g[:, 0:w], bg_[:, 0:w], AF.Sigmoid))
            T.op("dve", [bpb, gb], [gb], lambda: nc.vector.tensor_tensor(g[:, 0:w], bp[:, 0:w], g[:, 0:w], op=ALU.mult))
            T.op("dve", [gb, self.Hb[m][t.idx]], [self.Hb[m][t.idx]], lambda: nc.vector.tensor_tensor(
                self.H[:, m, t.h0:t.h0 + w], self.H[:, m, t.h0:t.h0 + w], g[:, 0:w], op=ALU.add))
            if last_sq and t.li == 1:
                T.op("act", [self.Hb[m][t.idx]], [self.SQXb[m]], lambda: nc.scalar.activation(
                    self.SQX[:, m, 0:w], self.H[:, m, t.h0:t.h0 + w], AF.Square))
            if hooks is not None and t.li == 0 and m in hooks:
                for f in hooks[m]:
                    f()
        self.run_phase([[("w_pg", l, 0, 8, m * 128), ("w_pp", l, 0, 2, m * 128)] for m in range(8)], tiles, ple,
                       post_tile=post_tile, s_tail=3, r_head=3, post_delay=1, flush_first=True)

    def final_tile(self, t, presq_x=False):
        nc, T = self.nc, self.T
        if t.samp:
            mode = False
        elif presq_x:
            mode = "x" if t.li == 1 else False
        else:
            mode = "xnow"
        self.rmsnorm([t], R_GFIN, final=True, presq=mode)
        if t.samp:
            for c in range(8):
                bk, bb = self.bank()
                T.op("pe", [self.Hb[c][t.idx], self.IDFb], [bb], lambda c=c, bk=bk: nc.tensor.transpose(
                    bk[0:16, 0:128], self.H[:, c, SEQ:TP], self.IDF[:]))
                T.op("act", [bb], [self.XSSb], lambda c=c, bk=bk: nc.scalar.copy(self.XSS[:, c * 128:(c + 1) * 128], bk[0:16, 0:128]))
            self.out_evs.append(T.dma("sp", [self.XSSb], [], self.ys, self.XSS[:]))
            return
        self.deferred_lo.append((self.final_store, t))

    def final_store(self, t):
        nc, T = self.nc, self.T
        for s in range(4):
            q = self.ystnext
            self.ystnext = (q + 1) % 4
            st, stb = self.XST[0][:, q, :], self.YSTb[q]
            for cg in range(2):
                bk, bb = self.bank()

                def emit(bk=bk, s=s, cg=cg):
                    last = None
                    for cc in range(4):
                        c = cg * 4 + cc
                        last = nc.tensor.transpose(bk[:, cc * 128:(cc + 1) * 128],
                                                   self.H[:, c, t.h0 + s * 128:t.h0 + (s + 1) * 128], self.IDF[:])
                    return last
                T.op("pe", [self.Hb[cg * 4 + cc][t.idx] for cc in range(4)] + [self.IDFb], [bb], emit)
                if (s + cg) % 2 == 0:
                    T.op("act", [bb], [stb], lambda bk=bk, cg=cg, st=st: nc.scalar.copy(st[:, cg * 512:(cg + 1) * 512], bk[:]))
                else:
                    T.op("dve", [bb], [stb], lambda bk=bk, cg=cg, st=st: nc.vector.tensor_copy(st[:, cg * 512:(cg + 1) * 512], bk[:]))
            self.out_evs.append(T.dma("sp", [stb], [], self.yp[t.h0 + s * 128:t.h0 + (s + 1) * 128, :], st))

    def program(self):
        self.build_identity()
        if not self.T.dry:
            self.w_pump()
        self.prologue()
        order = [(l, seg) for l in range(DEPTH) for seg in range(2)]
        self.state_shift_copies()
        for n, (l, seg) in enumerate(order):
            nxt = order[n + 1] if n + 1 < len(order) else None
            self.p1(l, seg, post_tile=lambda t, l=l: self.rmsnorm([t], R_GMLP + l, presq=True))
            bg = None
            if nxt is not None and nxt[0] != l:
                bg = {16 + 4 * j: (lambda nl=nxt[0], j=j: self.build_diag(nl, j, j)) for j in range(3)}
            self.p2(l, seg, post_tile=lambda t, l=l: self.rmsnorm([t], R_GPLE + l, presq=True), bg=bg)

            def after_p3(t, l=l, seg=seg, nxt=nxt):
                if nxt is not None:
                    nl, ns = nxt
                    nts = SEG_TILES[ns]
                    if t.li < 2:
                        self.rmsnorm([nts[t.li]], R_GMIX + nl, presq=("x" if t.li == 0 else "xnow"))
                        if t.li == 1 and len(nts) == 3 and len(SEG_TILES[seg]) == 2:
                            self.rmsnorm([nts[2]], R_GMIX + nl)
                    elif len(nts) == 3:
                        self.rmsnorm([nts[2]], R_GMIX + nl)
                if l == DEPTH - 1:
                    self.final_tile(t, presq_x=(nxt is None))
            hooks = None
            if nxt is not None and nxt[1] == 1:
                st = self.sample_state_steps(nxt[0])
                hooks = {}
                hooks.setdefault(-1, []).extend([st[0][0], st[1][0]])
                for k in range(len(st)):
                    hooks.setdefault(k, []).append(st[k][1])
                    if k + 2 < len(st):
                        hooks.setdefault(k, []).append(st[k + 2][0])
            pre_sq_tile = SEG_TILES[nxt[1]][0] if nxt is not None else None
            if n == 0:
                pst = pre_sq_tile
                pre_sq_tile = None
                hooks.setdefault(-1, []).append(lambda: self.load_x(range(2, 4), phase="dma"))
                hooks.setdefault(3, []).append(lambda: self.load_x(range(2, 3), phase="tr"))
                hooks.setdefault(3, []).append(lambda pst=pst: self.pre_squares(pst))
                hooks.setdefault(5, []).append(lambda: self.load_x(range(3, 4), phase="tr"))
                hooks.setdefault(6, []).append(lambda: self.load_x(range(0), sample=True, phase="tr"))
            self.p3(l, seg, post_tile=after_p3, hooks=hooks, pre_sq_tile=pre_sq_tile, last_sq=(nxt is None))
        while self.deferred or self.deferred_lo:
            self.flush_deferred()
        if not self.T.dry:
            self.T._wait("sp", [e for e in self.out_evs if e is not None])

    def build(self):
        self.declare()
        with self.es:
            self.alloc()
            self.T.dry = True
            self.program()
            self.pbnext = self.signext = self.rsnext = self.rlnext = self.sqnext = self.ystnext = 0
            self.out_evs = []
            self.T.dry = False
            self.program()
        return self.nc


_NC_CACHE = {}


def _get_nc():
    if "nc" not in _NC_CACHE:
        _NC_CACHE["nc"] = Builder().build()
    return _NC_CACHE["nc"]


def kernel(x_prompt, x_sample, state_conv_a, state_pool, state_conv_c, p_prompt, p_sample,
           norm_mix_g, w_in, conv_a_w, conv_a_b, ln_a_g, ln_a_b, pool_w, pool_scale, conv_c_w,
           w_out, norm_mlp_g, w_up, w_down, norm_ple_g, w_ple_gate, w_ple_proj, final_norm_g):
    f = lambda a: np.ascontiguousarray(np.asarray(a, dtype=np.float32))
    x_prompt, x_sample = f(x_prompt), f(x_sample)
    state_conv_a, state_pool, state_conv_c = f(state_conv_a), f(state_pool), f(state_conv_c)
    p_prompt, p_sample = f(p_prompt), f(p_sample)
    shared = {
        "g_mix": f(norm_mix_g), "g_mlp": f(norm_mlp_g), "g_ple": f(norm_ple_g), "g_fin": f(final_norm_g).reshape(1, D),
        "caw": f(conv_a_w), "cab": f(conv_a_b), "lng": f(ln_a_g), "lnb": f(ln_a_b),
        "plw": f(pool_w), "psc": f(pool_scale), "ccw": f(conv_c_w),
        "w_in": f(w_in), "w_out": f(w_out), "w_up": f(w_up), "w_down": f(w_down),
        "w_pg": f(w_ple_gate), "w_pp": f(w_ple_proj),
    }
    in_maps = []
    for c in range(NCORES):
        sl = slice(c * NS_TOK, (c + 1) * NS_TOK)
        m = dict(shared)
        m["xp"] = f(x_prompt[c]); m["xs"] = f(x_sample[sl, 0, :])
        m["sca"] = f(state_conv_a[:, sl]); m["spl"] = f(state_pool[:, sl]); m["scc"] = f(state_conv_c[:, sl])
        m["pp"] = f(p_prompt[:, c]); m["psm"] = f(p_sample[:, sl, 0, :])
        in_maps.append(m)
    nc = _get_nc()
    res = run_bass_kernel_spmd(nc, in_maps, core_ids=list(range(NCORES)))
    rs = res.results
    y_prompt = np.stack([rs[c]["yp"] for c in range(NCORES)], axis=0)
    y_sample = np.concatenate([rs[c]["ys"] for c in range(NCORES)], axis=0)[:, None, :]
    nap = np.stack([rs[c]["nap"] for c in range(NCORES)], axis=1)
    npp = np.stack([rs[c]["npp"] for c in range(NCORES)], axis=1)
    ncp = np.stack([rs[c]["ncp"] for c in range(NCORES)], axis=1)
    nas = np.concatenate([rs[c]["nas"] for c in range(NCORES)], axis=1)
    nps = np.concatenate([rs[c]["nps"] for c in range(NCORES)], axis=1)
    ncs = np.concatenate([rs[c]["ncs"] for c in range(NCORES)], axis=1)
    out = (y_prompt, y_sample, nap, npp, ncp, nas, nps, ncs)
    return tuple(np.ascontiguousarray(o, dtype=np.float32) for o in out)
```
